# Optimizing a Trainium2 kernel written in Bass

```python
import jax, jax.numpy as jnp
from jax import lax
import numpy as np

D_MODEL = 1024
BATCH = 8
SEQ = 8192
DEPTH = 1

CHUNK = 64
EPS = 1e-5
N_BRANCH = 2
GMLP_BLOCK = 128
GMLP_WIDTH = 1024
GMLP_GROUPS = 8
GMLP_GDIM = GMLP_WIDTH // GMLP_GROUPS
SSM_INNER = 2 * D_MODEL
SSM_HEAD_DIM = 64
SSM_HEADS = SSM_INNER // SSM_HEAD_DIM
SSM_GROUPS = 4
SSM_HPG = SSM_HEADS // SSM_GROUPS
SSM_STATE = 128
SSM_CONV = 4
SSM_CHUNK = CHUNK
SSM_XBC = SSM_INNER + 2 * SSM_GROUPS * SSM_STATE
D_FF = 2816
FFN_CONV = 3
IN_COLS = N_BRANCH * D_MODEL + 2 * GMLP_WIDTH + SSM_INNER + SSM_XBC + SSM_HEADS

kernel_name = "hybrid_gmlp_ssd_gated_merge_block"


def rmsnorm(x, w):
    xf = x.astype(jnp.float32)
    y = xf * lax.rsqrt(jnp.mean(xf * xf, axis=-1, keepdims=True) + EPS)
    return (y * w.astype(jnp.float32)).astype(x.dtype)


def causal_dwconv(x, w, b):
    K, C = w.shape
    y = lax.conv_general_dilated(
        x, w[:, None, :].astype(x.dtype), window_strides=(1,), padding=[(K - 1, 0)],
        dimension_numbers=('NWC', 'WIO', 'NWC'), feature_group_count=C)
    return y + b.astype(x.dtype)


def gmlp_mixer(za, ln_w, ln_b, w_s, b_s):
    Bsz, S, _ = za.shape
    z = jax.nn.gelu(za)
    u, v = jnp.split(z, 2, axis=-1)
    nb = S // GMLP_BLOCK
    v = v.reshape(Bsz, nb, GMLP_BLOCK, GMLP_GROUPS, GMLP_GDIM)
    vf = v.astype(jnp.float32)
    mu = jnp.mean(vf, axis=-1, keepdims=True)
    var = jnp.mean(jnp.square(vf - mu), axis=-1, keepdims=True)
    v = ((vf - mu) * lax.rsqrt(var + EPS) * ln_w.astype(jnp.float32)
         + ln_b.astype(jnp.float32)).astype(z.dtype)
    chunk_id = jnp.arange(GMLP_BLOCK) // CHUNK
    mask = chunk_id[None, :] <= chunk_id[:, None]
    ws = jnp.where(mask[None], w_s, 0).astype(v.dtype)
    sv = jnp.einsum('gts,bnsgd->bntgd', ws, v) + b_s.T.astype(v.dtype)[None, None, :, :, None]
    return u * sv.reshape(Bsz, S, GMLP_WIDTH)


def ssd_scan(x, dt, A, Bm, Cm):
    Bsz, S = x.shape[:2]
    L = SSM_CHUNK
    nc = S // L
    f32 = jnp.float32
    x = x.astype(f32).reshape(Bsz, nc, L, SSM_GROUPS, SSM_HPG, SSM_HEAD_DIM)
    dt = dt.reshape(Bsz, nc, L, SSM_GROUPS, SSM_HPG)
    Bm = Bm.astype(f32).reshape(Bsz, nc, L, SSM_GROUPS, SSM_STATE)
    Cm = Cm.astype(f32).reshape(Bsz, nc, L, SSM_GROUPS, SSM_STATE)
    a_cum = jnp.cumsum(dt * A.reshape(SSM_GROUPS, SSM_HPG), axis=2)
    xdt = x * dt[..., None]
    seg = a_cum[:, :, :, None] - a_cum[:, :, None, :]
    causal = jnp.tril(jnp.ones((L, L), dtype=bool))
    decay = jnp.exp(jnp.where(causal[:, :, None, None], seg, -jnp.inf))
    cb = jnp.einsum('bclgn,bcsgn->bclsg', Cm, Bm)
    y_diag = jnp.einsum('bclsg,bclsgh,bcsghp->bclghp', cb, decay, xdt)
    decay_end = jnp.exp(a_cum[:, :, -1:] - a_cum)
    states = jnp.einsum('bcsgn,bcsgh,bcsghp->bcghpn', Bm, decay_end, xdt)
    chunk_decay = jnp.exp(a_cum[:, :, -1])

    def step(h, inp):
        st, dec = inp
        return h * dec[..., None, None] + st, h

    h0 = jnp.zeros((Bsz, SSM_GROUPS, SSM_HPG, SSM_HEAD_DIM, SSM_STATE), f32)
    _, prev = lax.scan(step, h0, (jnp.moveaxis(states, 1, 0), jnp.moveaxis(chunk_decay, 1, 0)))
    prev = jnp.moveaxis(prev, 0, 1)
    y_off = jnp.einsum('bclgn,bcghpn,bclgh->bclghp', Cm, prev, jnp.exp(a_cum))
    return (y_diag + y_off).reshape(Bsz, S, SSM_HEADS, SSM_HEAD_DIM)


def mamba2_mixer(z, xbc, dt_raw, conv_w, conv_b, dt_bias, a_log, d_skip, norm_w):
    Bsz, S, _ = z.shape
    xbc = jax.nn.silu(causal_dwconv(xbc, conv_w, conv_b))
    xs, Bm, Cm = jnp.split(xbc, [SSM_INNER, SSM_INNER + SSM_GROUPS * SSM_STATE], axis=-1)
    xs = xs.reshape(Bsz, S, SSM_HEADS, SSM_HEAD_DIM)
    Bm = Bm.reshape(Bsz, S, SSM_GROUPS, SSM_STATE)
    Cm = Cm.reshape(Bsz, S, SSM_GROUPS, SSM_STATE)
    dt = jax.nn.softplus(dt_raw.astype(jnp.float32) + dt_bias.astype(jnp.float32))
    A = -jnp.exp(a_log.astype(jnp.float32))
    y = ssd_scan(xs, dt, A, Bm, Cm) + d_skip.astype(jnp.float32)[:, None] * xs.astype(jnp.float32)
    y = y.reshape(Bsz, S, SSM_INNER) * jax.nn.silu(z.astype(jnp.float32))
    yg = y.reshape(Bsz, S, SSM_GROUPS, SSM_INNER // SSM_GROUPS)
    yg = yg * lax.rsqrt(jnp.mean(yg * yg, axis=-1, keepdims=True) + EPS)
    y = yg.reshape(Bsz, S, SSM_INNER) * norm_w.astype(jnp.float32)
    return y.astype(z.dtype)


def setup_inputs(seed: int = 0) -> dict:
    key = jax.random.key(seed)
    ks = jax.random.split(key, 24)

    def nrm(k, shape, scale):
        return jax.random.normal(k, shape, jnp.float32) * scale

    def gain(k, shape):
        return 1.0 + 0.02 * jax.random.normal(k, shape, jnp.float32)

    dt0 = jnp.exp(jax.random.uniform(ks[12], (DEPTH, SSM_HEADS), jnp.float32,
                                     np.log(1e-3), np.log(1e-1)))
    dt_bias = dt0 + jnp.log(-jnp.expm1(-dt0))
    a_log = jnp.log(jax.random.uniform(ks[13], (DEPTH, SSM_HEADS), jnp.float32, 1.0, 16.0))
    return {
        "x": nrm(ks[0], (BATCH, SEQ, D_MODEL), 1.0),
        "mix_norm_w": gain(ks[1], (DEPTH, D_MODEL)),
        "w_in": nrm(ks[2], (DEPTH, D_MODEL, IN_COLS), D_MODEL ** -0.5),
        "gate_bias": nrm(ks[3], (DEPTH, N_BRANCH, D_MODEL), 0.01),
        "gmlp_ln_w": gain(ks[4], (DEPTH, GMLP_GROUPS, GMLP_GDIM)),
        "gmlp_ln_b": nrm(ks[5], (DEPTH, GMLP_GROUPS, GMLP_GDIM), 0.01),
        "gmlp_ws": nrm(ks[6], (DEPTH, GMLP_GROUPS, GMLP_BLOCK, GMLP_BLOCK), 0.5 * GMLP_BLOCK ** -0.5),
        "gmlp_bs": gain(ks[7], (DEPTH, GMLP_GROUPS, GMLP_BLOCK)),
        "ssm_conv_w": nrm(ks[8], (DEPTH, SSM_CONV, SSM_XBC), SSM_CONV ** -0.5),
        "ssm_conv_b": nrm(ks[9], (DEPTH, SSM_XBC), 0.01),
        "ssm_dt_bias": dt_bias,
        "ssm_a_log": a_log,
        "ssm_d": gain(ks[10], (DEPTH, SSM_HEADS)),
        "ssm_norm_w": gain(ks[11], (DEPTH, SSM_INNER)),
        "w_proj_a": nrm(ks[14], (DEPTH, GMLP_WIDTH, D_MODEL), GMLP_WIDTH ** -0.5),
        "w_proj_b": nrm(ks[15], (DEPTH, SSM_INNER, D_MODEL), SSM_INNER ** -0.5),
        "w_out": nrm(ks[16], (DEPTH, D_MODEL, D_MODEL), D_MODEL ** -0.5),
        "ffn_norm_w": gain(ks[17], (DEPTH, D_MODEL)),
        "ffn_w_up": nrm(ks[18], (DEPTH, D_MODEL, 2 * D_FF), D_MODEL ** -0.5),
        "ffn_conv_w": nrm(ks[19], (DEPTH, FFN_CONV, 2 * D_FF), FFN_CONV ** -0.5),
        "ffn_conv_b": nrm(ks[20], (DEPTH, 2 * D_FF), 0.01),
        "ffn_w_down": nrm(ks[21], (DEPTH, D_FF, D_MODEL), D_FF ** -0.5),
        "final_norm_w": gain(ks[22], (D_MODEL,)),
    }


def reference(x, mix_norm_w, w_in, gate_bias, gmlp_ln_w, gmlp_ln_b, gmlp_ws, gmlp_bs,
              ssm_conv_w, ssm_conv_b, ssm_dt_bias, ssm_a_log, ssm_d, ssm_norm_w,
              w_proj_a, w_proj_b, w_out, ffn_norm_w, ffn_w_up, ffn_conv_w, ffn_conv_b,
              ffn_w_down, final_norm_w):
    splits = [D_MODEL, 2 * D_MODEL, 2 * D_MODEL + 2 * GMLP_WIDTH,
              2 * D_MODEL + 2 * GMLP_WIDTH + SSM_INNER,
              2 * D_MODEL + 2 * GMLP_WIDTH + SSM_INNER + SSM_XBC]
    h = x
    for l in range(DEPTH):
        xn = rmsnorm(h, mix_norm_w[l])
        proj = xn @ w_in[l]
        g_a, g_b, za, z, xbc, dt_raw = jnp.split(proj, splits, axis=-1)
        y_a = gmlp_mixer(za, gmlp_ln_w[l], gmlp_ln_b[l], gmlp_ws[l], gmlp_bs[l]) @ w_proj_a[l]
        y_b = mamba2_mixer(z, xbc, dt_raw, ssm_conv_w[l], ssm_conv_b[l], ssm_dt_bias[l],
                           ssm_a_log[l], ssm_d[l], ssm_norm_w[l]) @ w_proj_b[l]
        merged = (jax.nn.sigmoid(g_a + gate_bias[l, 0]) * y_a
                  + jax.nn.sigmoid(g_b + gate_bias[l, 1]) * y_b)
        h = h + merged @ w_out[l]
        hn = rmsnorm(h, ffn_norm_w[l])
        up = causal_dwconv(hn @ ffn_w_up[l], ffn_conv_w[l], ffn_conv_b[l])
        gate, val = jnp.split(up, 2, axis=-1)
        h = h + (jax.nn.silu(gate) * val) @ ffn_w_down[l]
    return rmsnorm(h, final_norm_w)
```

```python
import numpy as np
import ml_dtypes
import concourse.bass as bass
import concourse.mybir as mybir
from concourse.bass_utils import run_bass_kernel_spmd

F32 = mybir.dt.float32
BF16 = mybir.dt.bfloat16
AF = mybir.ActivationFunctionType
ALU = mybir.AluOpType

D_MODEL = 1024
EPS = 1e-5
IN_COLS = 9248
D_FF = 2816
NSLOT = 3
NDSLOT = 2
ENGS = ("pe", "act", "dve", "pool", "sp")
WAIT_ALL_KEYS = {"const"}


class Res:
    __slots__ = ("name", "last_write", "readers")

    def __init__(self, name):
        self.name = name
        self.last_write = None
        self.readers = {}


class DmaSem:
    def __init__(self, key):
        self.key = key
        self.count = 0
        self.handle = None


class Tracker:
    def __init__(self):
        self.ops = {e: [] for e in ENGS}
        self.nops = {e: 0 for e in ENGS}
        self.seen = {e: {} for e in ENGS}
        self.dmasems = {}

    def dmasem(self, key):
        if key not in self.dmasems:
            self.dmasems[key] = DmaSem(key)
        return self.dmasems[key]

    def op(self, eng, fn, reads=(), writes=(), dma=None):
        idx = self.nops[eng] + 1
        deps = {}

        def need(tok, raw):
            if tok is None:
                return
            key, val = tok
            if dma is not None and key == dma.key:
                return
            if key == eng:
                if eng == "pe" or eng == "sp":
                    return
            if val > deps.get(key, 0):
                deps[key] = val

        for r in reads:
            need(r.last_write, True)
        for w in writes:
            need(w.last_write, False)
            for k, v in w.readers.items():
                need((k, v), False)
        seen = self.seen[eng]
        waits = []
        for k, v in deps.items():
            if v > seen.get(k, 0):
                waits.append((k, v))
                seen[k] = v
        if dma is not None:
            dma.count += 16
            tok = (dma.key, dma.count)
        else:
            tok = (eng, idx)
        for r in reads:
            if tok[1] > r.readers.get(tok[0], 0):
                r.readers[tok[0]] = tok[1]
        for w in writes:
            w.last_write = tok
            w.readers = {}
        self.nops[eng] = idx
        self.ops[eng].append((waits, fn, idx, dma))

    def handoff(self, old, new):
        merged = {}
        for r in old:
            if r.last_write is not None:
                k, v = r.last_write
                merged[k] = max(merged.get(k, 0), v)
            for k, v in r.readers.items():
                merged[k] = max(merged.get(k, 0), v)
        for r in new:
            r.last_write = None
            r.readers = dict(merged)

    def barrier(self):
        toks = {}
        for e in ENGS:
            if e != "sp" and self.nops[e] > 0:
                toks[e] = self.nops[e]
        for k, d in self.dmasems.items():
            if d.count > 0:
                toks[k] = d.count
        for e in ENGS:
            waits = []
            for k, v in toks.items():
                if k == e:
                    continue
                if v > self.seen[e].get(k, 0):
                    waits.append((k, v))
                    self.seen[e][k] = v
            if waits:
                self.ops[e].append((waits, None, None, None))

    def final_wait_all_dma(self, eng="sp"):
        waits = []
        for k, d in self.dmasems.items():
            if d.count > self.seen[eng].get(k, 0):
                waits.append((k, d.count))
                self.seen[eng][k] = d.count
        if waits:
            self.ops[eng].append((waits, None, None, None))

    def emit(self, nc, block, engsems):
        sig = {e: set() for e in ENGS}
        for e in ENGS:
            for waits, fn, idx, dma in self.ops[e]:
                for k, v in waits:
                    if k in sig:
                        sig[k].add(v)
        rank = {}
        for e in ENGS:
            s = sorted(sig[e])
            rank[e] = {v: i + 1 for i, v in enumerate(s)}
        tr = self

        def run(e_name, e):
            for waits, fn, idx, dma in tr.ops[e_name]:
                for k, v in waits:
                    if k in rank:
                        e.wait_ge(engsems[k], rank[k][v])
                    else:
                        if k in WAIT_ALL_KEYS:
                            v = tr.dmasems[k].count
                        e.wait_ge(tr.dmasems[k].handle, v)
                if fn is None:
                    continue
                ins = fn(e)
                if dma is not None:
                    ins.then_inc(dma.handle, 16)
                elif idx in rank[e_name]:
                    ins.then_inc(engsems[e_name], 1)

        @block.sync
        def _(e):
            run("sp", e)

        @block.tensor
        def _(e):
            run("pe", e)

        @block.scalar
        def _(e):
            run("act", e)

        @block.vector
        def _(e):
            run("dve", e)

        @block.gpsimd
        def _(e):
            run("pool", e)


def stream_units():
    U = []
    for ub in range(2):
        U.append(("win", 3072 + ub * 512))
    for ub in range(2):
        U.append(("win", 2048 + ub * 512))
    for ub in range(4):
        U.append(("win", 4096 + ub * 512))
    for ub in range(6):
        U.append(("win", 6144 + ub * 512))
    for jb in range(2):
        U.append(("win", jb * 512))
        U.append(("win", 1024 + jb * 512))
        U.append(("pa", jb * 512))
        U.append(("pb", (jb * 512, 0)))
        U.append(("pb", (jb * 512, 8)))
    for nb in range(2):
        U.append(("wo", nb * 512))
    for fu in range(11):
        U.append(("up", fu))
    for nb in range(2):
        for kp in range(3):
            U.append(("down", (nb * 512, kp)))
    return U


UNITS = stream_units()
NU = len(UNITS)
ND = 6 + 11


def build(S, dbg=False):
    assert S % 512 == 0
    NT = S // 512
    nc = bass.Bass("TRN2", target_bir_lowering=False)

    def din(name, shape, dt=F32):
        return nc.dram_tensor(name, list(shape), dt, kind="ExternalInput").ap()

    x = din("x", [S, 1024])
    w_in = din("w_in", [1024, IN_COLS])
    mix_norm_w = din("mix_norm_w", [8, 128])
    gate_bias = din("gate_bias", [16, 128])
    gmlp_ln_w = din("gmlp_ln_w", [8, 128])
    gmlp_ln_b = din("gmlp_ln_b", [1, 1024])
    gmlp_ws = din("gmlp_ws", [8, 128, 128])
    gmlp_bs = din("gmlp_bs", [1, 1024])
    ssm_conv_w = din("ssm_conv_w", [96, 128])
    ssm_conv_b = din("ssm_conv_b", [24, 128])
    ssm_dt_bias = din("ssm_dt_bias", [1, 32])
    ssm_a_log = din("ssm_a_log", [1, 32])
    ssm_d = din("ssm_d", [1, 32])
    ssm_norm_w = din("ssm_norm_w", [16, 128])
    w_proj_a = din("w_proj_a", [1024, 1024])
    w_proj_b = din("w_proj_b", [2048, 1024])
    w_out = din("w_out", [1024, 1024])
    ffn_norm_w = din("ffn_norm_w", [8, 128])
    ffn_w_up = din("ffn_w_up", [1024, 2 * D_FF])
    ffn_conv_w = din("ffn_conv_w", [132, 128])
    ffn_conv_b = din("ffn_conv_b", [44, 128])
    ffn_w_down = din("ffn_w_down", [D_FF, 1024])
    final_norm_w = din("final_norm_w", [1, 1024])
    c_ident = din("c_ident", [128, 128])
    c_tri = din("c_tri", [128, 128])
    c_upper = din("c_upper", [128, 128])
    c_ones = din("c_ones", [128, 128])
    c_gmask = din("c_gmask", [128, 128])
    out = nc.dram_tensor("out", [S, 1024], F32, kind="ExternalOutput").ap()
    wstream = nc.dram_tensor("wstream", [NU, 128, 4096], BF16, kind="Internal").ap()
    dstream = nc.dram_tensor("dstream", [ND, 128, 2048], BF16, kind="Internal").ap()
    dbg_out = None
    if dbg:
        dbg_out = nc.dram_tensor("dbg", [S, 1024], F32, kind="ExternalOutput").ap()

    T = Tracker()
    from contextlib import ExitStack
    es = ExitStack()

    def sb(name, shape, dt):
        return es.enter_context(nc.sbuf_tensor(name, list(shape), dt))

    def ps(name, shape, dt):
        return es.enter_context(nc.psum_tensor(name, list(shape), dt))

    with es:
        IDB = sb("IDB", [128, 128], BF16)
        IDF = sb("IDF", [128, 128], F32)
        TRI = sb("TRI", [128, 128], F32)
        UPPER = sb("UPPER", [128, 128], F32)
        ONES = sb("ONES", [128, 128], F32)
        WST = sb("WST", [128, 8, 128], BF16)
        QG = sb("QG", [128, 8, 128], F32)
        COLS = sb("COLS", [128, 512], F32)
        WDT = sb("WDT", [128, 8, 32], BF16)
        A_BC = sb("A_BC", [128, 32], F32)
        DTB_BC = sb("DTB_BC", [128, 32], F32)
        D_BC = sb("D_BC", [128, 32], F32)
        FNW_BC = sb("FNW_BC", [128, 1024], F32)
        NEGHALF = sb("NEGHALF", [128, 4], F32)
        WSLOT = [sb(f"WSLOT{i}", [128, 8, 512], BF16) for i in range(NSLOT)]
        DSLOT = [sb(f"DSLOT{i}", [128, 16, 128], BF16) for i in range(NDSLOT)]
        XH = sb("XH", [128, 4, 1024], F32)
        XNT = sb("XNT", [128, 8, 512], BF16)
        UT = sb("UT", [128, 8, 512], BF16)
        XY = sb("XY", [128, 16, 512], BF16)
        ZS = sb("ZS", [128, 4, 2048], BF16)
        BC = sb("BC", [128, 8, 512], BF16)
        VF = [sb(f"VF{i}", [128, 512], F32) for i in range(2)]
        PRE = [sb(f"PRE{i}", [128, 516], BF16) for i in range(3)]
        HISTX = sb("HISTX", [128, 24, 4], BF16)
        HISTF = sb("HISTF", [128, 44, 2], BF16)
        XNB = [sb(f"XNB{i}", [128, 1024], BF16) for i in range(2)]
        JUNKS = [sb(f"JUNK{i}", [128, 1024], BF16) for i in range(3)]
        SSQ = sb("SSQ", [128, 16], F32)
        MSQ = sb("MSQ", [128, 16], F32)
        RSTD = sb("RSTD", [128, 16], F32)
        BST = sb("BST", [128, 4, 6], F32)
        MV = sb("MV", [128, 4, 2], F32)
        VE = sb("VE", [128, 4], F32)
        RSTDV = sb("RSTDV", [128, 4], F32)
        DTR = sb("DTR", [128, 4, 32], F32)
        DTE = sb("DTE", [128, 4, 32], F32)
        DT = sb("DT", [128, 4, 32], F32)
        DTA = sb("DTA", [128, 4, 32], F32)
        EXPS = sb("EXPS", [128, 96], F32)
        DTD = sb("DTD", [128, 32], F32)
        SST = sb("SST", [128, 2048], F32)
        SBF = sb("SBF", [128, 2048], BF16)
        ARENA = sb("ARENA", [128, 6144], F32)

        def carve(off_bytes, shape, dt):
            n = int(np.prod(shape[1:]))
            esz = 4 if dt == F32 else 2
            nbytes = n * esz
            a = ARENA[:, off_bytes // 4:(off_bytes + nbytes) // 4]
            if dt != F32:
                a = a.bitcast(dt)
            return a, off_bytes + nbytes

        off = 0
        XS_TOK, off = carve(off, [128, 2048], BF16)
        XQ = []
        for i in range(2):
            a, off = carve(off, [128, 1024], F32)
            XQ.append(a)
        GSB = []
        for i in range(2):
            a, off = carve(off, [128, 512], BF16)
            GSB.append(a)
        XDT = []
        XDTD = []
        XSD = []
        for i in range(2):
            a, off = carve(off, [128, 512], BF16); XDT.append(a)
            a, off = carve(off, [128, 512], BF16); XDTD.append(a)
            a, off = carve(off, [128, 512], BF16); XSD.append(a)
        YZ = []
        for i in range(2):
            a, off = carve(off, [128, 512], F32); YZ.append(a)
        assert off <= 6144 * 4, off
        AT = ARENA[:, 0:22 * 256].bitcast(BF16).rearrange("p (j t) -> p j t", j=22)
        B_TOK = [sb(f"B_TOK{i}", [128, 512], BF16) for i in range(2)]
        CBM = sb("CBM", [128, 4, 128], BF16)
        DEC = [sb(f"DEC{i}", [128, 512], BF16) for i in range(2)]
        GT = [sb(f"GT{i}", [128, 4, 128], BF16) for i in range(3)]
        T1 = sb("T1", [128, 512], F32)
        TS = sb("TS", [128, 512], F32)
        YNB = sb("YNB", [128, 2048], BF16)
        SSQG = sb("SSQG", [128, 4], F32)
        MSG = sb("MSG", [128, 4], F32)
        RSTDG = sb("RSTDG", [128, 4], F32)
        SGA = sb("SGA", [128, 4, 512], BF16)
        SGB = sb("SGB", [128, 4, 512], BF16)
        TM = [sb(f"TM{i}", [128, 512], F32) for i in range(2)]
        GS = [sb(f"GS{i}", [128, 512], BF16) for i in range(2)]

        PA = [ps(f"PA{i}", [128, 512], F32) for i in range(2)]
        PT = ps("PT", [128, 1024], BF16)
        SEG = [ps(f"SEG{i}", [128, 512], F32) for i in range(2)]
        PY = ps("PY", [128, 512], F32)
        PW = ps("PW", [128, 512], F32)
        PM = ps("PM", [128, 512], F32)

        R = {}

        def res(name):
            if name not in R:
                R[name] = Res(name)
            return R[name]

        rXH = [res(f"XH{t}") for t in range(4)]
        rXNT = [res(f"XNT{t}") for t in range(4)]
        rUT = [res(f"UT{j}") for j in range(8)]
        rXY = [res(f"XY{t}") for t in range(4)]
        rZS = [res(f"ZS{t}") for t in range(4)]
        rBT = [res(f"BT{t}") for t in range(4)]
        rCT = [res(f"CT{t}") for t in range(4)]
        rMT = [res(f"MT{j}") for j in range(8)]
        rVN = [res(f"VN{t}") for t in range(4)]
        rWS = [res(f"WS{i}") for i in range(NSLOT)]
        rDS = [res(f"DS{i}") for i in range(NDSLOT)]
        rVF = [res(f"VF{i}") for i in range(2)]
        rPRE = [res(f"PRE{i}") for i in range(3)]
        rXNB = [res(f"XNB{i}") for i in range(2)]
        rPA = [res(f"PA{i}") for i in range(2)]
        rSEG = [res(f"SEG{i}") for i in range(2)]
        rXQ = [res(f"XQ{i}") for i in range(2)]
        rGSB = [res(f"GSB{i}") for i in range(2)]
        rXG = [res(f"XG{i}") for i in range(2)]
        rYZ = [res(f"YZ{i}") for i in range(2)]
        rBTOK = [res(f"BTOK{i}") for i in range(2)]
        rDEC = [res(f"DEC{i}") for i in range(2)]
        rGT = [res(f"GT{i}") for i in range(3)]
        rTM = [res(f"TM{i}") for i in range(2)]
        rGS = [res(f"GS{i}") for i in range(2)]
        rAT = [res(f"AT{j}") for j in range(22)]
        rSST = [res(f"SST{g}") for g in range(4)]
        rSBF = [res(f"SBF{g}") for g in range(4)]
        rHX = [res(f"HX{j}") for j in range(24)]
        rHF = [res(f"HF{j}") for j in range(44)]
        ARENA_SSD = [res("XSTOK")] + rXQ + rGSB + rXG + rYZ
        rXSTOK = res("XSTOK")

        ctr = {}

        def rot(name, n):
            v = ctr.get(name, 0)
            ctr[name] = v + 1
            return v % n

        def dma(out_ap, in_ap, reads, writes, semkey):
            d = T.dmasem(semkey)
            T.op("sp", lambda e, o=out_ap, i=in_ap: e.dma_start(out=o, in_=i), reads=reads, writes=writes, dma=d)

        def mm_group(out_ap, pairs, reads, writes, fp32=False):
            n = len(pairs)

            def fn(e, o=out_ap, pairs=pairs, n=n):
                ins = None
                for i, (l, r) in enumerate(pairs):
                    ins = e.matmul(o, lhsT=l, rhs=r, start=(i == 0), stop=(i == n - 1))
                return ins
            T.op("pe", fn, reads=reads, writes=writes)

        def transposes(dst_list, src_list, reads, writes, ident):
            def fn(e, d=dst_list, s=src_list, ident=ident):
                ins = None
                for o, i in zip(d, s):
                    ins = e.transpose(out=o, in_=i, identity=ident)
                return ins
            T.op("pe", fn, reads=reads, writes=writes)

        def act(out_ap, in_ap, func, reads, writes, bias=None, scale=None, accum_out=None):
            kw = {}
            if bias is not None:
                kw["bias"] = bias
            if scale is not None:
                kw["scale"] = scale
            if accum_out is not None:
                kw["accum_out"] = accum_out
            T.op("act", lambda e, o=out_ap, i=in_ap, f=func, kw=kw: e.activation(out=o, in_=i, func=f, **kw),
                 reads=reads, writes=writes)

        def tt(eng, out_ap, in0, in1, op, reads, writes):
            T.op(eng, lambda e, o=out_ap, a=in0, b=in1, op=op: e.tensor_tensor(out=o, in0=a, in1=b, op=op),
                 reads=reads, writes=writes)

        def tsc(eng, out_ap, in0, s1, s2, op0, op1, reads, writes):
            if s2 is None:
                T.op(eng, lambda e, o=out_ap, a=in0, s1=s1, op0=op0: e.tensor_scalar(out=o, in0=a, scalar1=s1, scalar2=None, op0=op0),
                     reads=reads, writes=writes)
            else:
                T.op(eng, lambda e, o=out_ap, a=in0, s1=s1, s2=s2, op0=op0, op1=op1:
                     e.tensor_scalar(out=o, in0=a, scalar1=s1, scalar2=s2, op0=op0, op1=op1),
                     reads=reads, writes=writes)

        def stt(out_ap, in0, scalar, in1, op0, op1, reads, writes):
            T.op("dve", lambda e, o=out_ap, a=in0, s=scalar, b=in1, op0=op0, op1=op1:
                 e.scalar_tensor_tensor(out=o, in0=a, scalar=s, in1=b, op0=op0, op1=op1),
                 reads=reads, writes=writes)

        def copy(eng, out_ap, in_ap, reads, writes):
            if eng == "act":
                act(out_ap, in_ap, AF.Copy, reads, writes)
            else:
                T.op(eng, lambda e, o=out_ap, i=in_ap: e.tensor_copy(out=o, in_=i), reads=reads, writes=writes)

        def memset(eng, ap, val, writes):
            T.op(eng, lambda e, a=ap, v=val: e.memset(a, v), reads=(), writes=writes)

        def rsqrt_small(dst, src, n, scale, reads_w, tmp):
            rs, ws_, rt = reads_w
            tsc("dve", tmp, src, scale, EPS, ALU.mult, ALU.add, reads=[rs], writes=[rt])
            tt("pool", dst, tmp, NEGHALF[:, 0:n], ALU.pow, reads=[rt], writes=[ws_])

        rC = res("CONST")
        rROWS = res("ROWS")
        STG = [XH[:].rearrange("p a b -> p (a b)").rearrange("p (k n) -> p k n", k=8),
               ZS[:].rearrange("p a b -> p (a b)").bitcast(F32).rearrange("p (k n) -> p k n", k=8)]
        rSTG = [res("STG0"), res("STG1")]
        STGB = [XY[:].rearrange("p a b -> p (a b)")[:, 0:4096], XY[:].rearrange("p a b -> p (a b)")[:, 4096:8192]]
        rSTGB = [res("STGB0"), res("STGB1")]
        ROWS = [UT[:].rearrange("p a b -> p (a b)").bitcast(F32)[:, i * 128:(i + 1) * 128] for i in range(4)]
        LNB_BC = BC[:].rearrange("p a b -> p (a b)").bitcast(F32)[:, 0:1024].rearrange("p (g d) -> p g d", g=8)
        BS_BC = BC[:].rearrange("p a b -> p (a b)").bitcast(F32)[:, 1024:2048].rearrange("p (g d) -> p g d", g=8)
        WSF = XNT[:].rearrange("p a b -> p (a b)").bitcast(F32)
        GMASK = WSF[:, 0:128]
        WSRAW = [WSF[:, 128 + i * 128:256 + i * 128] for i in range(2)]
        WSTF = [WSF[:, 512 + i * 128:640 + i * 128] for i in range(2)]
        WDTF = WSF[:, 1024:1280].rearrange("p (k n) -> p k n", k=8)
        rWSRAW = [res("WSRAW0"), res("WSRAW1")]
        rWSTF = [res("WSTF0"), res("WSTF1")]

        dma(IDF[:], c_ident, [], [rC], "const")
        dma(TRI[:], c_tri, [], [rC], "const")
        dma(UPPER[:], c_upper, [], [rC], "const")
        dma(ONES[:], c_ones, [], [rC], "const")
        dma(GMASK, c_gmask, [], [rC], "const")
        dma(A_BC[:], ssm_a_log.partition_broadcast(128), [], [rC], "const")
        dma(DTB_BC[:], ssm_dt_bias.partition_broadcast(128), [], [rC], "const")
        dma(D_BC[:], ssm_d.partition_broadcast(128), [], [rC], "const")
        dma(FNW_BC[:], final_norm_w.partition_broadcast(128), [], [rC], "const")
        dma(LNB_BC.rearrange("p g d -> p (g d)"), gmlp_ln_b.partition_broadcast(128), [], [rC], "const")
        dma(BS_BC.rearrange("p g d -> p (g d)"), gmlp_bs.partition_broadcast(128), [], [rC], "const")
        dma(WDTF, w_in.rearrange("(k p) n -> p k n", p=128)[:, :, 9216:9248], [], [rC], "const")
        for i in range(4):
            memset("pool", ROWS[i], 0.0, [rROWS])
        row_srcs = [(0, 0, mix_norm_w, 8), (0, 8, ffn_norm_w, 8), (0, 16, ssm_norm_w, 16), (0, 32, gate_bias, 16),
                    (0, 48, ssm_conv_b, 24), (0, 72, ffn_conv_b, 44), (0, 116, gmlp_ln_w, 8),
                    (1, 0, ssm_conv_w, 96)]
        for (ti, r0, src, n) in row_srcs:
            dma(ROWS[ti][r0:r0 + n, :], src, [], [rROWS], "const")
        dma(ROWS[2][:, :], ffn_conv_w[0:128, :], [], [rROWS], "const")
        dma(ROWS[3][0:4, :], ffn_conv_w[128:132, :], [], [rROWS], "const")
        memset("dve", NEGHALF[:], -0.5, [rC])
        memset("dve", HISTX[:], 0.0, rHX)
        memset("dve", HISTF[:], 0.0, rHF)
        memset("pool", SST[:], 0.0, rSST)
        memset("pool", SBF[:], 0.0, rSBF)
        copy("dve", IDB[:], IDF[:], [rC], [rC])
        act(A_BC[:], A_BC[:], AF.Exp, [rC], [rC])
        tsc("dve", A_BC[:], A_BC[:], -1.0, None, ALU.mult, None, [rC], [rC])
        for i in range(4):
            transposes([PA[0][:, i * 128:(i + 1) * 128]], [ROWS[i]], [rROWS, rC], [rPA[0]], IDF[:])
        copy("dve", COLS[:], PA[0][:], [rPA[0]], [rC])

        def col(c):
            return COLS[:, c:c + 1]
        mixw_cols = COLS[:, 0:8]
        ffnw_cols = COLS[:, 8:16]
        ssmnw_cols = COLS[:, 16:32]

        def gb_col(branch, j): return col(32 + branch * 8 + j)
        def convb_col(j): return col(48 + j)
        def ffnb_col(j): return col(72 + j)
        def lnw_col(g): return col(116 + g)
        def convw_col(k, j): return col(128 + k * 24 + j)
        def ffncw_col(k, j): return col(256 + k * 44 + j)

        tt("dve", WDT[:], WDTF, mixw_cols.unsqueeze(2).to_broadcast([128, 8, 32]), ALU.mult, [rC], [rC])
        for g in range(8):
            b = g % 2
            dma(WSRAW[b], gmlp_ws[g], [], [rWSRAW[b]], f"wsraw{b}")
            tt("dve", WSRAW[b], WSRAW[b], GMASK, ALU.mult, [rWSRAW[b], rC], [rWSRAW[b]])
            transposes([PA[1][:, 0:128]], [WSRAW[b]], [rWSRAW[b], rC], [rPA[1]], IDF[:])
            copy("dve", WSTF[b], PA[1][:, 0:128], [rPA[1]], [rWSTF[b]])
            copy("act", WST[:, g, :], WSTF[b], [rWSTF[b]], [rC])
            mm_group(PA[1][:, 128:256], [(LNB_BC[:, g, :], WSTF[b])], [rC, rWSTF[b]], [rPA[1]])
            tt("dve", QG[:, g, :], PA[1][:, 128:256], BS_BC[:, g, :], ALU.add, [rPA[1], rC], [rC])

        def unit_src(kind, arg):
            if kind == "win":
                v = w_in.rearrange("(k p) n -> p k n", p=128)
                return [((0, 8), (0, 512), v[:, :, arg:arg + 512])], mixw_cols, 8
            if kind == "pa":
                v = w_proj_a.rearrange("(k p) n -> p k n", p=128)
                return [((0, 8), (0, 512), v[:, :, arg:arg + 512])], None, 8
            if kind == "pb":
                c0, k0 = arg
                v = w_proj_b.rearrange("(k p) n -> p k n", p=128)
                return [((0, 8), (0, 512), v[:, k0:k0 + 8, c0:c0 + 512])], ssmnw_cols[:, k0:k0 + 8], 8
            if kind == "wo":
                v = w_out.rearrange("(k p) n -> p k n", p=128)
                return [((0, 8), (0, 512), v[:, :, arg:arg + 512])], None, 8
            if kind == "up":
                v = ffn_w_up.rearrange("(k p) n -> p k n", p=128)
                return [((0, 8), (0, 256), v[:, :, arg * 256:arg * 256 + 256]),
                        ((0, 8), (256, 512), v[:, :, D_FF + arg * 256:D_FF + arg * 256 + 256])], ffnw_cols, 8
            if kind == "down":
                c0, kp = arg
                v = ffn_w_down.rearrange("(k p) n -> p k n", p=128)
                kcs = 8 if kp < 2 else 6
                return [((0, kcs), (0, 512), v[:, kp * 8:kp * 8 + kcs, c0:c0 + 512])], None, kcs
            raise ValueError(kind)

        cast_engs = ["dve", "pool", "act"]
        for u, (kind, arg) in enumerate(UNITS):
            sbi = u % 2
            srcs, scale, kcs = unit_src(kind, arg)
            for (k0, k1), (c0, c1), src in srcs:
                dma(STG[sbi][:, k0:k1, c0:c1], src, [], [rSTG[sbi]], f"stg{sbi}")
            dstv = STGB[sbi].rearrange("p (k n) -> p k n", k=8)
            if scale is not None:
                eng = ["dve", "pool"][u % 2]
                tt(eng, dstv[:, 0:kcs, :], STG[sbi][:, 0:kcs, :], scale.unsqueeze(2).to_broadcast([128, kcs, 512]),
                   ALU.mult, [rSTG[sbi], rC], [rSTGB[sbi]])
            else:
                eng = cast_engs[u % 3]
                copy(eng, dstv[:, 0:kcs, :], STG[sbi][:, 0:kcs, :], [rSTG[sbi]], [rSTGB[sbi]])
            dma(wstream[u][:, 0:kcs * 512], STGB[sbi][:, 0:kcs * 512], [rSTGB[sbi]], [], f"stgb{sbi}")

        DGS = [SGA[:].rearrange("p a b -> p (a b)"), SGB[:].rearrange("p a b -> p (a b)")]
        rDGS = [res("DGS0"), res("DGS1")]
        for du in range(ND):
            b = du % 2
            dst = DGS[b].rearrange("p (m c) -> p m c", c=128)
            nmat = 16 if du < 6 else 12
            for mi in range(nmat):
                if du < 6:
                    jj, k = mi // 4, mi % 4
                    cc_ = convw_col(k, du * 4 + jj)
                else:
                    fu = du - 6
                    cc, k = mi // 3, mi % 3
                    ch = (2 * fu + cc) if cc < 2 else 22 + 2 * fu + (cc - 2)
                    cc_ = ffncw_col(k, ch)
                eng = ["dve", "pool"][mi % 2]
                tsc(eng, dst[:, mi, :], IDF[:], cc_, None, ALU.mult, None, [rC], [rDGS[b]])
            dma(dstream[du][:, 0:nmat * 128], DGS[b][:, 0:nmat * 128], [rDGS[b]], [], f"dgs{b}")

        T.barrier()
        for r in R.values():
            r.last_write = None
            r.readers = {}

        wpos = {"issued": 0, "dissued": 0}
        total_w = NT * NU
        total_d = NT * ND

        def w_ensure(upto):
            while wpos["issued"] < min(upto, total_w):
                q = wpos["issued"]
                u = q % NU
                s = q % NSLOT
                kind, arg = UNITS[u]
                kcs = 6 if (kind == "down" and arg[1] == 2) else 8
                dma(WSLOT[s][:].rearrange("p k n -> p (k n)")[:, 0:kcs * 512], wstream[u][:, 0:kcs * 512], [], [rWS[s]], f"ws{s}")
                wpos["issued"] += 1

        def d_ensure(upto):
            while wpos["dissued"] < min(upto, total_d):
                q = wpos["dissued"]
                du = q % ND
                s = q % NDSLOT
                nmat = 16 if du < 6 else 12
                dma(DSLOT[s][:].rearrange("p m c -> p (m c)")[:, 0:nmat * 128], dstream[du][:, 0:nmat * 128], [], [rDS[s]], f"ds{s}")
                wpos["dissued"] += 1

        wq = {"w": 0, "d": 0}

        def w_acquire(n=1):
            q = wq["w"]
            w_ensure(q + NSLOT)
            wq["w"] += n
            return [(WSLOT[(q + i) % NSLOT], rWS[(q + i) % NSLOT]) for i in range(n)]

        def d_acquire():
            q = wq["d"]
            d_ensure(q + NDSLOT)
            wq["d"] += 1
            return DSLOT[q % NDSLOT], rDS[q % NDSLOT]

        def rmsnorm_to_T(m, ssq_off):
            for t4 in range(4):
                jk = rot("junk", 3)
                act(JUNKS[jk][:], XH[:, t4, :], AF.Square, [rXH[t4]], [res(f"JUNK{jk}"), res(f"SSQ{ssq_off + t4}")],
                    accum_out=SSQ[:, ssq_off + t4:ssq_off + t4 + 1])
            for t4 in range(4):
                c = ssq_off + t4
                rsqrt_small(RSTD[:, c:c + 1], SSQ[:, c:c + 1], 1, 1.0 / 1024,
                            (res(f"SSQ{c}"), res(f"RSTD{c}"), res(f"MSQ{c}")), MSQ[:, c:c + 1])
            for t4 in range(4):
                c = ssq_off + t4
                b = rot("xnb", 2)
                act(XNB[b][:], XH[:, t4, :], AF.Identity, [rXH[t4], res(f"RSTD{c}")], [rXNB[b]], scale=RSTD[:, c:c + 1])
                transposes([PT[:, j * 128:(j + 1) * 128] for j in range(8)],
                           [XNB[b][:, j * 128:(j + 1) * 128] for j in range(8)],
                           [rXNB[b]], [res("PT")], IDB[:])
                copy("dve", XNT[:, :, t4 * 128:(t4 + 1) * 128], PT[:].rearrange("p (j t) -> p j t", j=8),
                     [res("PT")], [rXNT[t4]])

        for m in range(NT):
            tok0 = m * 512
            for t4 in range(4):
                dma(XH[:, t4, :], x[tok0 + t4 * 128:tok0 + (t4 + 1) * 128, :], [], [rXH[t4]], f"xh{t4}")
            T.handoff(rAT, ARENA_SSD)
            T.handoff(rMT, rBT + rCT)
            rmsnorm_to_T(m, 0)

            T.handoff(rXY, rVN)
            VN = XY[:].rearrange("p a b -> p (a b)")[:, 0:4096].rearrange("p (t c) -> p t c", t=4)
            for ub in range(2):
                (W, rW), = w_acquire()
                for t4 in range(4):
                    pb = rot("pa", 2)
                    mm_group(PA[pb][:], [(XNT[:, kc, t4 * 128:(t4 + 1) * 128], W[:, kc, :]) for kc in range(8)],
                             [rXNT[t4], rW], [rPA[pb]])
                    vb = rot("vf", 2)
                    act(VF[vb][:], PA[pb][:], AF.Gelu, [rPA[pb]], [rVF[vb]])
                    for g4 in range(4):
                        T.op("dve", lambda e, o=BST[:, g4, :], i=VF[vb][:, g4 * 128:(g4 + 1) * 128]: e.bn_stats(out=o, in_=i),
                             reads=[rVF[vb]], writes=[res(f"BST{g4}")])
                        T.op("dve", lambda e, o=MV[:, g4, :], i=BST[:, g4, :]: e.bn_aggr(out=o, in_=i),
                             reads=[res(f"BST{g4}")], writes=[res("MV")])
                    tsc("dve", VE[:], MV[:, :, 1], EPS, None, ALU.add, None, [res("MV")], [res("VE")])
                    tt("pool", RSTDV[:], VE[:], NEGHALF[:, 0:4], ALU.pow, [res("VE")], [res("RSTDV")])
                    for g4 in range(4):
                        tsc("pool", VN[:, t4, ub * 512 + g4 * 128:ub * 512 + (g4 + 1) * 128], VF[vb][:, g4 * 128:(g4 + 1) * 128],
                            MV[:, g4, 0:1], RSTDV[:, g4:g4 + 1], ALU.subtract, ALU.mult,
                            [rVF[vb], res("MV"), res("RSTDV")], [rVN[t4]])
            for ub in range(2):
                (W, rW), = w_acquire()
                for jj in range(4):
                    j = ub * 4 + jj
                    pb = rot("pa", 2)
                    mm_group(PA[pb][:], [(W[:, kc, jj * 128:(jj + 1) * 128], XNT[:, kc, :]) for kc in range(8)],
                             rXNT + [rW], [rPA[pb]])
                    act(UT[:, j, :], PA[pb][:], AF.Gelu, [rPA[pb]], [rUT[j]])
            for g in range(8):
                pb = rot("pa", 2)

                def fn(e, pb=pb, g=g):
                    ins = None
                    for t4 in range(4):
                        ins = e.matmul(PA[pb][:, t4 * 128:(t4 + 1) * 128], lhsT=VN[:, t4, g * 128:(g + 1) * 128],
                                       rhs=WST[:, g, :], start=True, stop=True)
                    return ins
                T.op("pe", fn, reads=rVN, writes=[rPA[pb]])
                tb = rot("tm", 2)
                stt(TM[tb][:].rearrange("p (t c) -> p t c", t=4), PA[pb][:].rearrange("p (t c) -> p t c", t=4),
                    lnw_col(g), QG[:, g, :].unsqueeze(1).to_broadcast([128, 4, 128]), ALU.mult, ALU.add,
                    [rPA[pb]], [rTM[tb]])
                tt("pool", UT[:, g, :], TM[tb][:], UT[:, g, :], ALU.mult, [rTM[tb], rUT[g]], [rUT[g]])
            T.handoff(rVN, rXY)
            for ub in range(4):
                (W, rW), = w_acquire()
                for t4 in range(4):
                    pb = rot("pa", 2)
                    mm_group(PA[pb][:], [(XNT[:, kc, t4 * 128:(t4 + 1) * 128], W[:, kc, :]) for kc in range(8)],
                             [rXNT[t4], rW], [rPA[pb]])
                    act(ZS[:, t4, ub * 512:(ub + 1) * 512], PA[pb][:], AF.Silu, [rPA[pb]], [rZS[t4]])
            for ub in range(6):
                (W, rW), = w_acquire()
                DG, rDG = d_acquire()
                for jj in range(4):
                    j = ub * 4 + jj
                    pb = rot("pa", 2)
                    mm_group(PA[pb][:], [(W[:, kc, jj * 128:(jj + 1) * 128], XNT[:, kc, :]) for kc in range(8)],
                             rXNT + [rW], [rPA[pb]])
                    pr = rot("pre", 3)
                    act(PRE[pr][:, 4:516], PA[pb][:], AF.Copy, [rPA[pb]], [rPRE[pr]])
                    copy("pool", PRE[pr][:, 0:4], HISTX[:, j, :], [rHX[j]], [rPRE[pr]])
                    copy("pool", HISTX[:, j, :], PRE[pr][:, 512:516], [rPRE[pr]], [rHX[j]])
                    pb2 = rot("pa", 2)
                    mm_group(PA[pb2][:], [(DG[:, jj * 4 + k, :], PRE[pr][:, 1 + k:1 + k + 512]) for k in range(4)],
                             [rDG, rPRE[pr]], [rPA[pb2]])
                    if j < 16:
                        dst, rd = XY[:, j, :], rXY
                    elif j < 20:
                        dst, rd = BC[:, j - 16, :], rBT
                    else:
                        dst, rd = BC[:, 4 + j - 20, :], rCT
                    act(dst, PA[pb2][:], AF.Silu, [rPA[pb2]], rd, bias=convb_col(j))
            for t4 in range(4):
                mm_group(PM[:, 256:288], [(XNT[:, kc, t4 * 128:(t4 + 1) * 128], WDT[:, kc, :]) for kc in range(8)],
                         [rXNT[t4]], [res("PM")])
                tt("dve", DTR[:, t4, :], PM[:, 256:288], DTB_BC[:], ALU.add, [res("PM")], [res("DTR")])
            act(DTE[:], DTR[:], AF.Exp, [res("DTR")], [res("DTE")])
            act(DT[:], DTE[:], AF.Ln, [res("DTE")], [res("DT")], bias=1.0)
            tt("dve", DTA[:], DT[:], A_BC[:].unsqueeze(1).to_broadcast([128, 4, 32]), ALU.mult, [res("DT")], [res("DTA")])

            for t4 in range(4):
                tsl = slice(t4 * 128, (t4 + 1) * 128)
                for r2 in range(2):
                    transposes([PT[:, j * 128:(j + 1) * 128] for j in range(8)],
                               [XY[:, r2 * 8 + j, tsl] for j in range(8)], [rXY[t4]], [res("PT")], IDB[:])
                    copy("act" if r2 == 0 else "dve", XS_TOK[:, r2 * 1024:(r2 + 1) * 1024], PT[:], [res("PT")], [rXSTOK])
                bb = rot("btok", 2)
                transposes([PT[:, g * 128:(g + 1) * 128] for g in range(4)],
                           [BC[:, g, tsl] for g in range(4)], [rBT[t4]], [res("PT")], IDB[:])
                copy("act", B_TOK[bb][:], PT[:, 0:512], [res("PT")], [rBTOK[bb]])
                def fn_scan(e, t4=t4):
                    e.matmul(PM[:, 256:288], lhsT=TRI[:], rhs=DTA[:, t4, :], start=True, stop=True)
                    e.matmul(PM[:, 288:320], lhsT=UPPER[:], rhs=DTA[:, t4, :], start=True, stop=True)
                    return e.matmul(PM[:, 320:352], lhsT=ONES[:], rhs=DTA[:, t4, :], start=True, stop=True)
                T.op("pe", fn_scan, reads=[res("DTA")], writes=[res("PM")])
                act(EXPS[:], PM[:, 256:352], AF.Exp, [res("PM")], [res("EXPS")])
                tt("dve", DTD[:], DT[:, t4, :], EXPS[:, 32:64], ALU.mult, [res("DT"), res("EXPS")], [res("DTD")])
                for half in range(2):
                    def fn_cb(e, half=half, tsl=tsl):
                        ins = None
                        for gg in range(2):
                            g = half * 2 + gg
                            ins = e.matmul(PM[:, gg * 128:(gg + 1) * 128], lhsT=BC[:, g, tsl], rhs=BC[:, 4 + g, tsl],
                                           start=True, stop=True)
                        return ins
                    T.op("pe", fn_cb, reads=[rBT[t4], rCT[t4]], writes=[res("PM")])
                    tt("dve", CBM[:, half * 2:half * 2 + 2, :], PM[:, 0:256].rearrange("p (g l) -> p g l", g=2),
                       TRI[:].unsqueeze(1).to_broadcast([128, 2, 128]), ALU.mult, [res("PM")], [res("CBM")])
                for g in range(4):
                    gsl = slice(g * 512, (g + 1) * 512)
                    xg = rot("xg", 2)
                    xsv = XS_TOK[:, gsl].rearrange("p (h c) -> p h c", h=8)
                    tt("pool", XDT[xg][:].rearrange("p (h c) -> p h c", h=8), xsv,
                       DT[:, t4, g * 8:(g + 1) * 8].unsqueeze(2).to_broadcast([128, 8, 64]), ALU.mult,
                       [rXSTOK, res("DT")], [rXG[xg]])
                    tt("pool", XDTD[xg][:].rearrange("p (h c) -> p h c", h=8), xsv,
                       DTD[:, g * 8:(g + 1) * 8].unsqueeze(2).to_broadcast([128, 8, 64]), ALU.mult,
                       [rXSTOK, res("DTD")], [rXG[xg]])
                    tt("pool", XSD[xg][:].rearrange("p (h c) -> p h c", h=8), xsv,
                       D_BC[:, g * 8:(g + 1) * 8].unsqueeze(2).to_broadcast([128, 8, 64]), ALU.mult,
                       [rXSTOK], [rXG[xg]])
                    xq = rot("xq", 2)
                    tt("pool", XQ[xq][:].rearrange("p (h l) -> p h l", h=8),
                       DTA[:, t4, g * 8:(g + 1) * 8].unsqueeze(2).to_broadcast([128, 8, 128]),
                       TRI[:].unsqueeze(1).to_broadcast([128, 8, 128]), ALU.mult, [res("DTA")], [rXQ[xq]])
                    T.op("pe", lambda e, xg=xg: e.matmul(PY[:], lhsT=IDB[:], rhs=XSD[xg][:], start=True, stop=False),
                         reads=[rXG[xg], res("PY")], writes=[res("PY")])
                    for hh in range(2):
                        sgb = rot("seg", 2)
                        T.op("pe", lambda e, sgb=sgb, xq=xq, hh=hh: e.matmul(SEG[sgb][:], lhsT=UPPER[:],
                                                                                rhs=XQ[xq][:, hh * 512:(hh + 1) * 512],
                                                                                start=True, stop=True),
                             reads=[rXQ[xq]], writes=[rSEG[sgb]])
                        db = rot("dec", 2)
                        act(DEC[db][:], SEG[sgb][:], AF.Exp, [rSEG[sgb]], [rDEC[db]])
                        gtb = rot("gt", 3)
                        tt("dve", GT[gtb][:], DEC[db][:].rearrange("p (h l) -> p h l", h=4),
                           CBM[:, g, :].unsqueeze(1).to_broadcast([128, 4, 128]), ALU.mult,
                           [rDEC[db], res("CBM")], [rGT[gtb]])

                        def fn_y(e, gtb=gtb, hh=hh, xg=xg):
                            ins = None
                            for h4 in range(4):
                                h8 = hh * 4 + h4
                                ins = e.matmul(PY[:, h8 * 64:(h8 + 1) * 64], lhsT=GT[gtb][:, h4, :],
                                               rhs=XDT[xg][:, h8 * 64:(h8 + 1) * 64], start=False, stop=(h8 == 7))
                            return ins
                        T.op("pe", fn_y, reads=[rGT[gtb], rXG[xg], res("PY")], writes=[res("PY")])
                    T.op("pe", lambda e, g=g, tsl=tsl, gsl=gsl: e.matmul(PW[:], lhsT=BC[:, 4 + g, tsl], rhs=SBF[:, gsl],
                                                                           start=True, stop=True),
                         reads=[rCT[t4], rSBF[g]], writes=[res("PW")])
                    tt("dve", T1[:].rearrange("p (h c) -> p h c", h=8), PW[:].rearrange("p (h c) -> p h c", h=8),
                       EXPS[:, g * 8:(g + 1) * 8].unsqueeze(2).to_broadcast([128, 8, 64]), ALU.mult,
                       [res("PW"), res("EXPS")], [res("T1")])
                    yb = rot("yz", 2)
                    tt("dve", YZ[yb][:], PY[:], T1[:], ALU.add, [res("PY"), res("T1")], [rYZ[yb]])
                    tt("pool", YZ[yb][:], YZ[yb][:], ZS[:, t4, gsl], ALU.mult, [rYZ[yb], rZS[t4]], [rYZ[yb]])
                    jk = rot("junk", 3)
                    act(JUNKS[jk][:, 0:512], YZ[yb][:], AF.Square, [rYZ[yb]], [res(f"JUNK{jk}"), res(f"SSQG{g}")],
                        accum_out=SSQG[:, g:g + 1])
                    rsqrt_small(RSTDG[:, g:g + 1], SSQG[:, g:g + 1], 1, 1.0 / 512,
                                (res(f"SSQG{g}"), res(f"RSTDG{g}"), res(f"MSG{g}")), MSG[:, g:g + 1])
                    tsc("dve", YNB[:, gsl], YZ[yb][:], RSTDG[:, g:g + 1], None, ALU.mult, None,
                        [rYZ[yb], res(f"RSTDG{g}")], [res(f"YNB{g}")])
                    pb = rot("pa", 2)
                    T.op("pe", lambda e, pb=pb, bb=bb, g=g, xg=xg: e.matmul(PA[pb][:], lhsT=B_TOK[bb][:, g * 128:(g + 1) * 128],
                                                                              rhs=XDTD[xg][:], start=True, stop=True),
                         reads=[rBTOK[bb], rXG[xg]], writes=[rPA[pb]])
                    tt("pool", TS[:].rearrange("p (h c) -> p h c", h=8), SST[:, gsl].rearrange("p (h c) -> p h c", h=8),
                       EXPS[:, 64 + g * 8:64 + (g + 1) * 8].unsqueeze(2).to_broadcast([128, 8, 64]), ALU.mult,
                       [rSST[g], res("EXPS")], [res("TS")])
                    tt("dve", SST[:, gsl], TS[:], PA[pb][:], ALU.add, [res("TS"), rPA[pb]], [rSST[g]])
                    act(SBF[:, gsl], SST[:, gsl], AF.Copy, [rSST[g]], [rSBF[g]])
                for r2 in range(2):
                    transposes([PT[:, j * 128:(j + 1) * 128] for j in range(8)],
                               [YNB[:, (r2 * 8 + j) * 128:(r2 * 8 + j + 1) * 128] for j in range(8)],
                               [res(f"YNB{g}") for g in range(4)], [res("PT")], IDB[:])
                    copy("act" if r2 == 0 else "dve", XY[:, r2 * 8:(r2 + 1) * 8, tsl], PT[:].rearrange("p (j t) -> p j t", j=8),
                         [res("PT")], [rXY[t4]])

            T.handoff(rBT + rCT, rMT)
            MT = BC
            for jb in range(2):
                for br, SG in ((0, SGA), (1, SGB)):
                    (W, rW), = w_acquire()
                    for jj in range(4):
                        pb = rot("pa", 2)
                        mm_group(PA[pb][:], [(W[:, kc, jj * 128:(jj + 1) * 128], XNT[:, kc, :]) for kc in range(8)],
                                 rXNT + [rW], [rPA[pb]])
                        act(SG[:, jj, :], PA[pb][:], AF.Sigmoid, [rPA[pb]], [res(f"SG{br}_{jj}")], bias=gb_col(br, jb * 4 + jj))
                (Wa, rWa), (Wb0, rWb0), (Wb1, rWb1) = w_acquire(3)
                for jj in range(4):
                    j = jb * 4 + jj
                    pb = rot("pa", 2)
                    mm_group(PA[pb][:], [(Wa[:, kc, jj * 128:(jj + 1) * 128], UT[:, kc, :]) for kc in range(8)],
                             rUT + [rWa], [rPA[pb]])
                    tb = rot("tm", 2)
                    tt("dve", TM[tb][:], PA[pb][:], SGA[:, jj, :], ALU.mult, [rPA[pb], res(f"SG0_{jj}")], [rTM[tb]])
                    pb2 = rot("pa", 2)
                    mm_group(PA[pb2][:], [((Wb0 if kc < 8 else Wb1)[:, kc % 8, jj * 128:(jj + 1) * 128], XY[:, kc, :])
                                          for kc in range(16)], rXY + [rWb0, rWb1], [rPA[pb2]])
                    tb2 = rot("tm", 2)
                    tt("dve", TM[tb2][:], PA[pb2][:], SGB[:, jj, :], ALU.mult, [rPA[pb2], res(f"SG1_{jj}")], [rTM[tb2]])
                    tt("pool", MT[:, j, :], TM[tb][:], TM[tb2][:], ALU.add, [rTM[tb], rTM[tb2]], [rMT[j]])
            for nb in range(2):
                (W, rW), = w_acquire()
                for t4 in range(4):
                    pb = rot("pa", 2)
                    mm_group(PA[pb][:], [(MT[:, kc, t4 * 128:(t4 + 1) * 128], W[:, kc, :]) for kc in range(8)],
                             rMT + [rW], [rPA[pb]])
                    tt("dve", XH[:, t4, nb * 512:(nb + 1) * 512], PA[pb][:], XH[:, t4, nb * 512:(nb + 1) * 512], ALU.add,
                       [rPA[pb], rXH[t4]], [rXH[t4]])
            if dbg:
                for t4 in range(4):
                    dma(dbg_out[tok0 + t4 * 128:tok0 + (t4 + 1) * 128, :], XH[:, t4, :], [rXH[t4]], [], f"dbg{t4}")
            rmsnorm_to_T(m, 4)
            T.handoff(ARENA_SSD, rAT)
            for fu in range(11):
                (W, rW), = w_acquire()
                DG, rDG = d_acquire()
                for cc in range(4):
                    ch = (2 * fu + cc) if cc < 2 else 22 + 2 * fu + (cc - 2)
                    pb = rot("pa", 2)
                    mm_group(PA[pb][:], [(W[:, kc, cc * 128:(cc + 1) * 128], XNT[:, kc, :]) for kc in range(8)],
                             rXNT + [rW], [rPA[pb]])
                    pr = rot("pre", 3)
                    act(PRE[pr][:, 2:514], PA[pb][:], AF.Copy, [rPA[pb]], [rPRE[pr]])
                    copy("pool", PRE[pr][:, 0:2], HISTF[:, ch, :], [rHF[ch]], [rPRE[pr]])
                    copy("pool", HISTF[:, ch, :], PRE[pr][:, 512:514], [rPRE[pr]], [rHF[ch]])
                    pb2 = rot("pa", 2)
                    mm_group(PA[pb2][:], [(DG[:, cc * 3 + k, :], PRE[pr][:, k:k + 512]) for k in range(3)],
                             [rDG, rPRE[pr]], [rPA[pb2]])
                    if cc < 2:
                        act(GS[cc][:], PA[pb2][:], AF.Silu, [rPA[pb2]], [rGS[cc]], bias=ffnb_col(ch))
                    else:
                        jat = 2 * fu + (cc - 2)
                        stt(AT[:, jat, :], PA[pb2][:], ffnb_col(ch), GS[cc - 2][:], ALU.add, ALU.mult,
                            [rPA[pb2], rGS[cc - 2]], [rAT[jat]])
            for nb in range(2):
                slots = w_acquire(3)
                for t4 in range(4):
                    pb = rot("pa", 2)
                    pairs = []
                    for kc in range(22):
                        Wk, _ = slots[kc // 8]
                        pairs.append((AT[:, kc, t4 * 128:(t4 + 1) * 128], Wk[:, kc % 8, :]))
                    mm_group(PA[pb][:], pairs, rAT + [s[1] for s in slots], [rPA[pb]])
                    tt("dve", XH[:, t4, nb * 512:(nb + 1) * 512], PA[pb][:], XH[:, t4, nb * 512:(nb + 1) * 512], ALU.add,
                       [rPA[pb], rXH[t4]], [rXH[t4]])
            for t4 in range(4):
                c = 8 + t4
                jk = rot("junk", 3)
                act(JUNKS[jk][:], XH[:, t4, :], AF.Square, [rXH[t4]], [res(f"JUNK{jk}"), res(f"SSQ{c}")], accum_out=SSQ[:, c:c + 1])
                rsqrt_small(RSTD[:, c:c + 1], SSQ[:, c:c + 1], 1, 1.0 / 1024,
                            (res(f"SSQ{c}"), res(f"RSTD{c}"), res(f"MSQ{c}")), MSQ[:, c:c + 1])
                stt(XH[:, t4, :], XH[:, t4, :], RSTD[:, c:c + 1], FNW_BC[:], ALU.mult, ALU.mult,
                    [rXH[t4], res(f"RSTD{c}")], [rXH[t4]])
                dma(out[tok0 + t4 * 128:tok0 + (t4 + 1) * 128, :], XH[:, t4, :], [rXH[t4]], [], f"xo{t4}")

        T.final_wait_all_dma("sp")

        sem_es = ExitStack()
        with sem_es:
            engsems = {e: sem_es.enter_context(nc.semaphore(f"s_{e}")) for e in ENGS if e != "sp"}
            for k, d in T.dmasems.items():
                d.handle = sem_es.enter_context(nc.semaphore(f"d_{k}"))
            with nc.Block() as block:
                T.emit(nc, block, engsems)
    return nc


def _consts():
    i = np.arange(128)
    ident = np.eye(128, dtype=np.float32)
    tri = (i[:, None] <= i[None, :]).astype(np.float32)
    upper = (i[:, None] > i[None, :]).astype(np.float32)
    ones = np.ones((128, 128), np.float32)
    cid = i // 64
    gmask = (cid[None, :] <= cid[:, None]).astype(np.float32)
    return dict(c_ident=ident, c_tri=tri, c_upper=upper, c_ones=ones, c_gmask=gmask)


def _core_inputs(inp, b, S):
    f = lambda a: np.ascontiguousarray(np.asarray(a, dtype=np.float32))
    d = dict(
        x=f(inp["x"][b, :S]),
        w_in=f(inp["w_in"][0]),
        mix_norm_w=f(inp["mix_norm_w"][0]).reshape(8, 128),
        gate_bias=f(inp["gate_bias"][0]).reshape(16, 128),
        gmlp_ln_w=f(inp["gmlp_ln_w"][0]).reshape(8, 128),
        gmlp_ln_b=f(inp["gmlp_ln_b"][0]).reshape(1, 1024),
        gmlp_ws=f(inp["gmlp_ws"][0]),
        gmlp_bs=f(inp["gmlp_bs"][0]).reshape(1, 1024),
        ssm_conv_w=f(inp["ssm_conv_w"][0]).reshape(96, 128),
        ssm_conv_b=f(inp["ssm_conv_b"][0]).reshape(24, 128),
        ssm_dt_bias=f(inp["ssm_dt_bias"][0]).reshape(1, 32),
        ssm_a_log=f(inp["ssm_a_log"][0]).reshape(1, 32),
        ssm_d=f(inp["ssm_d"][0]).reshape(1, 32),
        ssm_norm_w=f(inp["ssm_norm_w"][0]).reshape(16, 128),
        w_proj_a=f(inp["w_proj_a"][0]),
        w_proj_b=f(inp["w_proj_b"][0]),
        w_out=f(inp["w_out"][0]),
        ffn_norm_w=f(inp["ffn_norm_w"][0]).reshape(8, 128),
        ffn_w_up=f(inp["ffn_w_up"][0]),
        ffn_conv_w=f(inp["ffn_conv_w"][0]).reshape(132, 128),
        ffn_conv_b=f(inp["ffn_conv_b"][0]).reshape(44, 128),
        ffn_w_down=f(inp["ffn_w_down"][0]),
        final_norm_w=f(inp["final_norm_w"]).reshape(1, 1024),
    )
    d.update(_consts())
    return d


def run(inputs, S=None, n_cores=8, dbg=False, trace=False):
    B = inputs["x"].shape[0]
    if S is None:
        S = inputs["x"].shape[1]
    nc = build(S, dbg=dbg)
    in_maps = [_core_inputs(inputs, b, S) for b in range(n_cores)]
    res = run_bass_kernel_spmd(nc, in_maps, core_ids=list(range(n_cores)), trace=trace)
    outs = np.stack([np.asarray(r["out"], dtype=np.float32) for r in res.results], axis=0)
    if dbg:
        return outs, np.stack([np.asarray(r["dbg"], dtype=np.float32) for r in res.results], axis=0), res
    return outs


def kernel(**inputs):
    return run(inputs, n_cores=inputs["x"].shape[0])
```

```python
import numpy as np
import ml_dtypes
import concourse.bass as bass
import concourse.mybir as mybir
from concourse.bass_utils import run_bass_kernel_spmd

F32 = mybir.dt.float32
BF16 = mybir.dt.bfloat16
AF = mybir.ActivationFunctionType
ALU = mybir.AluOpType

D_MODEL = 1024
EPS = 1e-5
IN_COLS = 9248
D_FF = 2816
NSLOT = 3
NDSLOT = 2
ENGS = ("pe", "act", "dve", "pool", "sp")
WAIT_ALL_KEYS = {"const"}


class Res:
    __slots__ = ("name", "last_write", "readers")

    def __init__(self, name):
        self.name = name
        self.last_write = None
        self.readers = {}


class DmaSem:
    def __init__(self, key):
        self.key = key
        self.count = 0
        self.handle = None


class Tracker:
    def __init__(self):
        self.ops = {e: [] for e in ENGS}
        self.nops = {e: 0 for e in ENGS}
        self.seen = {e: {} for e in ENGS}
        self.dmasems = {}

    def dmasem(self, key):
        if key not in self.dmasems:
            self.dmasems[key] = DmaSem(key)
        return self.dmasems[key]

    def op(self, eng, fn, reads=(), writes=(), dma=None):
        idx = self.nops[eng] + 1
        deps = {}

        def need(tok, raw):
            if tok is None:
                return
            key, val = tok
            if dma is not None and key == dma.key:
                return
            if key == eng:
                if eng == "pe" or eng == "sp":
                    return
            if val > deps.get(key, 0):
                deps[key] = val

        for r in reads:
            need(r.last_write, True)
        for w in writes:
            need(w.last_write, False)
            for k, v in w.readers.items():
                need((k, v), False)
        seen = self.seen[eng]
        waits = []
        for k, v in deps.items():
            if v > seen.get(k, 0):
                waits.append((k, v))
                seen[k] = v
        if dma is not None:
            dma.count += 16
            tok = (dma.key, dma.count)
        else:
            tok = (eng, idx)
        for r in reads:
            if tok[1] > r.readers.get(tok[0], 0):
                r.readers[tok[0]] = tok[1]
        for w in writes:
            w.last_write = tok
            w.readers = {}
        self.nops[eng] = idx
        self.ops[eng].append((waits, fn, idx, dma))

    def handoff(self, old, new):
        merged = {}
        for r in old:
            if r.last_write is not None:
                k, v = r.last_write
                merged[k] = max(merged.get(k, 0), v)
            for k, v in r.readers.items():
                merged[k] = max(merged.get(k, 0), v)
        for r in new:
            r.last_write = None
            r.readers = dict(merged)

    def barrier(self):
        toks = {}
        for e in ENGS:
            if e != "sp" and self.nops[e] > 0:
                toks[e] = self.nops[e]
        for k, d in self.dmasems.items():
            if d.count > 0:
                toks[k] = d.count
        for e in ENGS:
            waits = []
            for k, v in toks.items():
                if k == e:
                    continue
                if v > self.seen[e].get(k, 0):
                    waits.append((k, v))
                    self.seen[e][k] = v
            if waits:
                self.ops[e].append((waits, None, None, None))

    def final_wait_all_dma(self, eng="sp"):
        waits = []
        for k, d in self.dmasems.items():
            if d.count > self.seen[eng].get(k, 0):
                waits.append((k, d.count))
                self.seen[eng][k] = d.count
        if waits:
            self.ops[eng].append((waits, None, None, None))

    def emit(self, nc, block, engsems):
        sig = {e: set() for e in ENGS}
        for e in ENGS:
            for waits, fn, idx, dma in self.ops[e]:
                for k, v in waits:
                    if k in sig:
                        sig[k].add(v)
        rank = {}
        for e in ENGS:
            s = sorted(sig[e])
            rank[e] = {v: i + 1 for i, v in enumerate(s)}
        tr = self

        def run(e_name, e):
            for waits, fn, idx, dma in tr.ops[e_name]:
                wl = []
                for k, v in waits:
                    if k in rank:
                        wl.append((engsems[k], rank[k][v]))
                    else:
                        if k in WAIT_ALL_KEYS:
                            v = tr.dmasems[k].count
                        wl.append((tr.dmasems[k].handle, v))
                embed = None
                if fn is not None and wl and e_name in ("act", "dve", "pool", "pe"):
                    embed = wl.pop()
                for sm, v in wl:
                    e.wait_ge(sm, v)
                if fn is None:
                    continue
                ret = fn(e)
                if isinstance(ret, tuple):
                    first, ins = ret
                else:
                    first = ins = ret
                if embed is not None:
                    first._wait_ge(embed[0], embed[1])
                if dma is not None:
                    ins.then_inc(dma.handle, 16)
                elif idx in rank[e_name]:
                    ins.then_inc(engsems[e_name], 1)

        @block.sync
        def _(e):
            run("sp", e)

        @block.tensor
        def _(e):
            run("pe", e)

        @block.scalar
        def _(e):
            run("act", e)

        @block.vector
        def _(e):
            run("dve", e)

        @block.gpsimd
        def _(e):
            run("pool", e)


def stream_units():
    U = []
    for ub in range(2):
        U.append(("win", 3072 + ub * 512))
        U.append(("win", 2048 + ub * 512))
    for ub in range(4):
        U.append(("win", 4096 + ub * 512))
    for ub in range(6):
        U.append(("win", 6144 + ub * 512))
    for jb in range(2):
        U.append(("win", jb * 512))
        U.append(("win", 1024 + jb * 512))
        U.append(("pa", jb * 512))
        U.append(("pb", (jb * 512, 0)))
        U.append(("pb", (jb * 512, 8)))
    for nb in range(2):
        U.append(("wo", nb * 512))
    for fu in range(11):
        U.append(("up", fu))
    for nb in range(2):
        for kp in range(3):
            U.append(("down", (nb * 512, kp)))
    return U


UNITS = stream_units()
NU = len(UNITS)
ND = 6 + 11


def build(S, dbg=False):
    assert S % 512 == 0
    NT = S // 512
    nc = bass.Bass("TRN2", target_bir_lowering=False)

    def din(name, shape, dt=F32):
        return nc.dram_tensor(name, list(shape), dt, kind="ExternalInput").ap()

    x = din("x", [S, 1024])
    w_in = din("w_in", [1024, IN_COLS])
    mix_norm_w = din("mix_norm_w", [8, 128])
    gate_bias = din("gate_bias", [16, 128])
    gmlp_ln_w = din("gmlp_ln_w", [8, 128])
    gmlp_ln_b = din("gmlp_ln_b", [1, 1024])
    gmlp_ws = din("gmlp_ws", [8, 128, 128])
    gmlp_bs = din("gmlp_bs", [1, 1024])
    ssm_conv_w = din("ssm_conv_w", [96, 128])
    ssm_conv_b = din("ssm_conv_b", [24, 128])
    ssm_dt_bias = din("ssm_dt_bias", [1, 32])
    ssm_a_log = din("ssm_a_log", [1, 32])
    ssm_d = din("ssm_d", [1, 32])
    ssm_norm_w = din("ssm_norm_w", [16, 128])
    w_proj_a = din("w_proj_a", [1024, 1024])
    w_proj_b = din("w_proj_b", [2048, 1024])
    w_out = din("w_out", [1024, 1024])
    ffn_norm_w = din("ffn_norm_w", [8, 128])
    ffn_w_up = din("ffn_w_up", [1024, 2 * D_FF])
    ffn_conv_w = din("ffn_conv_w", [132, 128])
    ffn_conv_b = din("ffn_conv_b", [44, 128])
    ffn_w_down = din("ffn_w_down", [D_FF, 1024])
    final_norm_w = din("final_norm_w", [1, 1024])
    c_ident = din("c_ident", [128, 128])
    c_tri = din("c_tri", [128, 128])
    c_upper = din("c_upper", [128, 128])
    c_ones = din("c_ones", [128, 128])
    c_gmask = din("c_gmask", [128, 128])
    out = nc.dram_tensor("out", [S, 1024], F32, kind="ExternalOutput").ap()
    wstream = nc.dram_tensor("wstream", [NU, 128, 4096], BF16, kind="Internal").ap()
    dstream = nc.dram_tensor("dstream", [ND, 128, 2048], BF16, kind="Internal").ap()
    dbg_out = None
    if dbg:
        dbg_out = nc.dram_tensor("dbg", [S, 1024], F32, kind="ExternalOutput").ap()

    T = Tracker()
    from contextlib import ExitStack
    es = ExitStack()

    def sb(name, shape, dt):
        return es.enter_context(nc.sbuf_tensor(name, list(shape), dt))

    def ps(name, shape, dt):
        return es.enter_context(nc.psum_tensor(name, list(shape), dt))

    with es:
        IDB = sb("IDB", [128, 128], BF16)
        IDF = sb("IDF", [128, 128], F32)
        TRI = sb("TRI", [128, 128], F32)
        UPPER = sb("UPPER", [128, 128], F32)
        ONES = sb("ONES", [128, 128], F32)
        WST = sb("WST", [128, 8, 128], BF16)
        QG = sb("QG", [128, 8, 128], F32)
        COLS = sb("COLS", [128, 512], F32)
        WDT = sb("WDT", [128, 8, 32], BF16)
        A_BC = sb("A_BC", [128, 32], F32)
        DTB_BC = sb("DTB_BC", [128, 32], F32)
        D_BC = sb("D_BC", [128, 32], F32)
        FNW_BC = sb("FNW_BC", [128, 1024], F32)
        NEGHALF = sb("NEGHALF", [128, 4], F32)
        WSLOT = [sb(f"WSLOT{i}", [128, 8, 512], BF16) for i in range(NSLOT)]
        DSLOT = [sb(f"DSLOT{i}", [128, 16, 128], BF16) for i in range(NDSLOT)]
        XH = sb("XH", [128, 4, 1024], F32)
        XNT = sb("XNT", [128, 8, 512], BF16)
        UT = sb("UT", [128, 8, 512], BF16)
        XY = sb("XY", [128, 16, 512], BF16)
        ZS = sb("ZS", [128, 4, 2048], BF16)
        BC = sb("BC", [128, 8, 512], BF16)
        VF = [sb(f"VF{i}", [128, 512], F32) for i in range(4)]
        PRE = [sb(f"PRE{i}", [128, 516], BF16) for i in range(3)]
        HISTX = sb("HISTX", [128, 24, 4], BF16)
        HISTF = sb("HISTF", [128, 44, 2], BF16)
        XNB = [sb(f"XNB{i}", [128, 1024], BF16) for i in range(2)]
        JUNKS = [sb(f"JUNK{i}", [128, 1024], BF16) for i in range(2)]
        SSQ = sb("SSQ", [128, 16], F32)
        MSQ = sb("MSQ", [128, 16], F32)
        RSTD = sb("RSTD", [128, 16], F32)
        BST = [sb(f"BST{i}", [128, 4, 6], F32) for i in range(2)]
        MV = [sb(f"MV{i}", [128, 4, 2], F32) for i in range(2)]
        VE = [sb(f"VE{i}", [128, 4], F32) for i in range(2)]
        RSTDV = [sb(f"RSTDV{i}", [128, 4], F32) for i in range(2)]
        DTR = sb("DTR", [128, 4, 32], F32)
        DTE = sb("DTE", [128, 4, 32], F32)
        DT = sb("DT", [128, 4, 32], F32)
        DTA = sb("DTA", [128, 4, 32], F32)
        EXPS = [sb(f"EXPS{i}", [128, 96], F32) for i in range(2)]
        DTD = [sb(f"DTD{i}", [128, 32], F32) for i in range(2)]
        SST = sb("SST", [128, 2048], F32)
        SBF = sb("SBF", [128, 2048], BF16)
        ARENA = sb("ARENA", [128, 6144], F32)

        def carve(off_bytes, shape, dt):
            n = int(np.prod(shape[1:]))
            esz = 4 if dt == F32 else 2
            nbytes = n * esz
            a = ARENA[:, off_bytes // 4:(off_bytes + nbytes) // 4]
            if dt != F32:
                a = a.bitcast(dt)
            return a, off_bytes + nbytes

        off = 0
        XS_TOK = []
        for i in range(2):
            a, off = carve(off, [128, 2048], BF16)
            XS_TOK.append(a)
        XQ = []
        for i in range(2):
            a, off = carve(off, [128, 1024], F32)
            XQ.append(a)
        XDT = []
        XDTD = []
        XSD = []
        for i in range(2):
            a, off = carve(off, [128, 512], BF16); XDT.append(a)
            a, off = carve(off, [128, 512], BF16); XDTD.append(a)
            a, off = carve(off, [128, 512], BF16); XSD.append(a)
        YZ = []
        a, off = carve(off, [128, 512], F32); YZ.append(a)
        YZ.append(sb("YZ1", [128, 512], F32)[:])
        assert off <= 6144 * 4, off
        AT = ARENA[:, 0:22 * 256].bitcast(BF16).rearrange("p (j t) -> p j t", j=22)
        B_TOK = [sb(f"B_TOK{i}", [128, 512], BF16) for i in range(2)]
        CBM = [sb(f"CBM{i}", [128, 4, 128], BF16) for i in range(2)]
        DEC = [sb(f"DEC{i}", [128, 512], BF16) for i in range(2)]
        GT = [sb(f"GT{i}", [128, 4, 128], BF16) for i in range(3)]
        SSDX = sb("SSDX", [128, 4096], BF16)
        YNB = SSDX[:, 0:2048]
        T1 = SSDX[:, 2048:3072].bitcast(F32)
        TS = SSDX[:, 3072:4096].bitcast(F32)
        SGA = SSDX[:, 0:2048].rearrange("p (a b) -> p a b", a=4)
        SGB = SSDX[:, 2048:4096].rearrange("p (a b) -> p a b", a=4)
        SSQG = sb("SSQG", [128, 4], F32)
        MSG = sb("MSG", [128, 4], F32)
        RSTDG = sb("RSTDG", [128, 4], F32)
        TM = [sb(f"TM{i}", [128, 512], F32) for i in range(2)]
        GS = [sb(f"GS{i}", [128, 512], BF16) for i in range(2)]

        PA = [ps(f"PA{i}", [128, 512], F32) for i in range(2)]
        PT = ps("PT", [128, 1024], BF16)
        SEG = [ps(f"SEG{i}", [128, 512], F32) for i in range(2)]
        PY = [ps(f"PY{i}", [128, 512], F32) for i in range(2)]
        PM = ps("PM", [128, 512], F32)

        R = {}

        def res(name):
            if name not in R:
                R[name] = Res(name)
            return R[name]

        rXH = [res(f"XH{t}") for t in range(4)]
        rXNT = [res(f"XNT{t}") for t in range(4)]
        rUT = [res(f"UT{j}") for j in range(8)]
        rXY = [res(f"XY{t}") for t in range(4)]
        rZS = [res(f"ZS{t}") for t in range(4)]
        rBT = [res(f"BT{t}") for t in range(4)]
        rCT = [res(f"CT{t}") for t in range(4)]
        rMT = [res(f"MT{j}") for j in range(8)]
        rVN = [res(f"VN{t}") for t in range(4)]
        rWS = [res(f"WS{i}") for i in range(NSLOT)]
        rDS = [res(f"DS{i}") for i in range(NDSLOT)]
        rVF = [res(f"VF{i}") for i in range(4)]
        rPY = [res(f"PY{i}") for i in range(2)]
        rPRE = [res(f"PRE{i}") for i in range(3)]
        rXNB = [res(f"XNB{i}") for i in range(2)]
        rPA = [res(f"PA{i}") for i in range(2)]
        rSEG = [res(f"SEG{i}") for i in range(2)]
        rXQ = [res(f"XQ{i}") for i in range(2)]
        rXG = [res(f"XG{i}") for i in range(2)]
        rYZ = [res(f"YZ{i}") for i in range(2)]
        rBTOK = [res(f"BTOK{i}") for i in range(2)]
        rDEC = [res(f"DEC{i}") for i in range(2)]
        rGT = [res(f"GT{i}") for i in range(3)]
        rTM = [res(f"TM{i}") for i in range(2)]
        rGS = [res(f"GS{i}") for i in range(2)]
        rAT = [res(f"AT{j}") for j in range(22)]
        rSST = [res(f"SST{g}") for g in range(4)]
        rSBF = [res(f"SBF{g}") for g in range(4)]
        rHX = [res(f"HX{j}") for j in range(24)]
        rHF = [res(f"HF{j}") for j in range(44)]
        rXSTOK = [res("XSTOK0"), res("XSTOK1")]
        ARENA_SSD = rXSTOK + rXQ + rXG + [rYZ[0]]

        ctr = {}

        def rot(name, n):
            v = ctr.get(name, 0)
            ctr[name] = v + 1
            return v % n

        def dma(out_ap, in_ap, reads, writes, semkey):
            d = T.dmasem(semkey)
            T.op("sp", lambda e, o=out_ap, i=in_ap: e.dma_start(out=o, in_=i), reads=reads, writes=writes, dma=d)

        def mm_group(out_ap, pairs, reads, writes, fp32=False):
            n = len(pairs)

            def fn(e, o=out_ap, pairs=pairs, n=n):
                ins = None
                first = None
                for i, (l, r) in enumerate(pairs):
                    ins = e.matmul(o, lhsT=l, rhs=r, start=(i == 0), stop=(i == n - 1))
                    if first is None:
                        first = ins
                return first, ins
            T.op("pe", fn, reads=reads, writes=writes)

        def transposes(dst_list, src_list, reads, writes, ident):
            def fn(e, d=dst_list, s=src_list, ident=ident):
                ins = None
                first = None
                for o, i in zip(d, s):
                    ins = e.transpose(out=o, in_=i, identity=ident)
                    if first is None:
                        first = ins
                return first, ins
            T.op("pe", fn, reads=reads, writes=writes)

        def act(out_ap, in_ap, func, reads, writes, bias=None, scale=None, accum_out=None):
            kw = {}
            if bias is not None:
                kw["bias"] = bias
            if scale is not None:
                kw["scale"] = scale
            if accum_out is not None:
                kw["accum_out"] = accum_out
            T.op("act", lambda e, o=out_ap, i=in_ap, f=func, kw=kw: e.activation(out=o, in_=i, func=f, **kw),
                 reads=reads, writes=writes)

        def tt(eng, out_ap, in0, in1, op, reads, writes):
            T.op(eng, lambda e, o=out_ap, a=in0, b=in1, op=op: e.tensor_tensor(out=o, in0=a, in1=b, op=op),
                 reads=reads, writes=writes)

        def tsc(eng, out_ap, in0, s1, s2, op0, op1, reads, writes):
            if s2 is None:
                T.op(eng, lambda e, o=out_ap, a=in0, s1=s1, op0=op0: e.tensor_scalar(out=o, in0=a, scalar1=s1, scalar2=None, op0=op0),
                     reads=reads, writes=writes)
            else:
                T.op(eng, lambda e, o=out_ap, a=in0, s1=s1, s2=s2, op0=op0, op1=op1:
                     e.tensor_scalar(out=o, in0=a, scalar1=s1, scalar2=s2, op0=op0, op1=op1),
                     reads=reads, writes=writes)

        def stt(out_ap, in0, scalar, in1, op0, op1, reads, writes):
            T.op("dve", lambda e, o=out_ap, a=in0, s=scalar, b=in1, op0=op0, op1=op1:
                 e.scalar_tensor_tensor(out=o, in0=a, scalar=s, in1=b, op0=op0, op1=op1),
                 reads=reads, writes=writes)

        def copy(eng, out_ap, in_ap, reads, writes):
            if eng == "act":
                act(out_ap, in_ap, AF.Copy, reads, writes)
            else:
                T.op(eng, lambda e, o=out_ap, i=in_ap: e.tensor_copy(out=o, in_=i), reads=reads, writes=writes)

        def memset(eng, ap, val, writes):
            T.op(eng, lambda e, a=ap, v=val: e.memset(a, v), reads=(), writes=writes)

        def rsqrt_small(dst, src, n, scale, reads_w, tmp):
            rs, ws_, rt = reads_w
            tsc("dve", tmp, src, scale, EPS, ALU.mult, ALU.add, reads=[rs], writes=[rt])
            tt("pool", dst, tmp, NEGHALF[:, 0:n], ALU.pow, reads=[rt], writes=[ws_])

        rC = res("CONST")
        rROWS = res("ROWS")
        STG = [XH[:].rearrange("p a b -> p (a b)").rearrange("p (k n) -> p k n", k=8),
               ZS[:].rearrange("p a b -> p (a b)").bitcast(F32).rearrange("p (k n) -> p k n", k=8)]
        rSTG = [res("STG0"), res("STG1")]
        STGB = [XY[:].rearrange("p a b -> p (a b)")[:, 0:4096], XY[:].rearrange("p a b -> p (a b)")[:, 4096:8192]]
        rSTGB = [res("STGB0"), res("STGB1")]
        ROWS = [UT[:].rearrange("p a b -> p (a b)").bitcast(F32)[:, i * 128:(i + 1) * 128] for i in range(4)]
        LNB_BC = BC[:].rearrange("p a b -> p (a b)").bitcast(F32)[:, 0:1024].rearrange("p (g d) -> p g d", g=8)
        BS_BC = BC[:].rearrange("p a b -> p (a b)").bitcast(F32)[:, 1024:2048].rearrange("p (g d) -> p g d", g=8)
        WSF = XNT[:].rearrange("p a b -> p (a b)").bitcast(F32)
        GMASK = WSF[:, 0:128]
        WSRAW = [WSF[:, 128 + i * 128:256 + i * 128] for i in range(2)]
        WSTF = [WSF[:, 512 + i * 128:640 + i * 128] for i in range(2)]
        WDTF = WSF[:, 1024:1280].rearrange("p (k n) -> p k n", k=8)
        rWSRAW = [res("WSRAW0"), res("WSRAW1")]
        rWSTF = [res("WSTF0"), res("WSTF1")]

        dma(IDF[:], c_ident, [], [rC], "const")
        dma(TRI[:], c_tri, [], [rC], "const")
        dma(UPPER[:], c_upper, [], [rC], "const")
        dma(ONES[:], c_ones, [], [rC], "const")
        dma(GMASK, c_gmask, [], [rC], "const")
        dma(A_BC[:], ssm_a_log.partition_broadcast(128), [], [rC], "const")
        dma(DTB_BC[:], ssm_dt_bias.partition_broadcast(128), [], [rC], "const")
        dma(D_BC[:], ssm_d.partition_broadcast(128), [], [rC], "const")
        dma(FNW_BC[:], final_norm_w.partition_broadcast(128), [], [rC], "const")
        dma(LNB_BC.rearrange("p g d -> p (g d)"), gmlp_ln_b.partition_broadcast(128), [], [rC], "const")
        dma(BS_BC.rearrange("p g d -> p (g d)"), gmlp_bs.partition_broadcast(128), [], [rC], "const")
        dma(WDTF, w_in.rearrange("(k p) n -> p k n", p=128)[:, :, 9216:9248], [], [rC], "const")
        for i in range(4):
            memset("pool", ROWS[i], 0.0, [rROWS])
        row_srcs = [(0, 0, mix_norm_w, 8), (0, 8, ffn_norm_w, 8), (0, 16, ssm_norm_w, 16), (0, 32, gate_bias, 16),
                    (0, 48, ssm_conv_b, 24), (0, 72, ffn_conv_b, 44), (0, 116, gmlp_ln_w, 8),
                    (1, 0, ssm_conv_w, 96)]
        for (ti, r0, src, n) in row_srcs:
            dma(ROWS[ti][r0:r0 + n, :], src, [], [rROWS], "const")
        dma(ROWS[2][:, :], ffn_conv_w[0:128, :], [], [rROWS], "const")
        dma(ROWS[3][0:4, :], ffn_conv_w[128:132, :], [], [rROWS], "const")
        memset("dve", NEGHALF[:], -0.5, [rC])
        memset("dve", HISTX[:], 0.0, rHX)
        memset("dve", HISTF[:], 0.0, rHF)
        memset("pool", SST[:], 0.0, rSST)
        memset("pool", SBF[:], 0.0, rSBF)
        copy("dve", IDB[:], IDF[:], [rC], [rC])
        act(A_BC[:], A_BC[:], AF.Exp, [rC], [rC])
        tsc("dve", A_BC[:], A_BC[:], -1.0, None, ALU.mult, None, [rC], [rC])
        for i in range(4):
            transposes([PA[0][:, i * 128:(i + 1) * 128]], [ROWS[i]], [rROWS, rC], [rPA[0]], IDF[:])
        copy("dve", COLS[:], PA[0][:], [rPA[0]], [rC])

        def col(c):
            return COLS[:, c:c + 1]
        mixw_cols = COLS[:, 0:8]
        ffnw_cols = COLS[:, 8:16]
        ssmnw_cols = COLS[:, 16:32]

        def gb_col(branch, j): return col(32 + branch * 8 + j)
        def convb_col(j): return col(48 + j)
        def ffnb_col(j): return col(72 + j)
        def lnw_col(g): return col(116 + g)
        def convw_col(k, j): return col(128 + k * 24 + j)
        def ffncw_col(k, j): return col(256 + k * 44 + j)

        tt("dve", WDT[:], WDTF, mixw_cols.unsqueeze(2).to_broadcast([128, 8, 32]), ALU.mult, [rC], [rC])
        for g in range(8):
            b = g % 2
            dma(WSRAW[b], gmlp_ws[g], [], [rWSRAW[b]], f"wsraw{b}")
            tt("dve", WSRAW[b], WSRAW[b], GMASK, ALU.mult, [rWSRAW[b], rC], [rWSRAW[b]])
            transposes([PA[1][:, 0:128]], [WSRAW[b]], [rWSRAW[b], rC], [rPA[1]], IDF[:])
            copy("dve", WSTF[b], PA[1][:, 0:128], [rPA[1]], [rWSTF[b]])
            copy("act", WST[:, g, :], WSTF[b], [rWSTF[b]], [rC])
            mm_group(PA[1][:, 128:256], [(LNB_BC[:, g, :], WSTF[b])], [rC, rWSTF[b]], [rPA[1]])
            tt("dve", QG[:, g, :], PA[1][:, 128:256], BS_BC[:, g, :], ALU.add, [rPA[1], rC], [rC])

        def unit_src(kind, arg):
            if kind == "win":
                v = w_in.rearrange("(k p) n -> p k n", p=128)
                return [((0, 8), (0, 512), v[:, :, arg:arg + 512])], mixw_cols, 8
            if kind == "pa":
                v = w_proj_a.rearrange("(k p) n -> p k n", p=128)
                return [((0, 8), (0, 512), v[:, :, arg:arg + 512])], None, 8
            if kind == "pb":
                c0, k0 = arg
                v = w_proj_b.rearrange("(k p) n -> p k n", p=128)
                return [((0, 8), (0, 512), v[:, k0:k0 + 8, c0:c0 + 512])], ssmnw_cols[:, k0:k0 + 8], 8
            if kind == "wo":
                v = w_out.rearrange("(k p) n -> p k n", p=128)
                return [((0, 8), (0, 512), v[:, :, arg:arg + 512])], None, 8
            if kind == "up":
                v = ffn_w_up.rearrange("(k p) n -> p k n", p=128)
                return [((0, 8), (0, 256), v[:, :, arg * 256:arg * 256 + 256]),
                        ((0, 8), (256, 512), v[:, :, D_FF + arg * 256:D_FF + arg * 256 + 256])], ffnw_cols, 8
            if kind == "down":
                c0, kp = arg
                v = ffn_w_down.rearrange("(k p) n -> p k n", p=128)
                kcs = 8 if kp < 2 else 6
                return [((0, kcs), (0, 512), v[:, kp * 8:kp * 8 + kcs, c0:c0 + 512])], None, kcs
            raise ValueError(kind)

        cast_engs = ["dve", "pool", "act"]
        for u, (kind, arg) in enumerate(UNITS):
            sbi = u % 2
            srcs, scale, kcs = unit_src(kind, arg)
            for (k0, k1), (c0, c1), src in srcs:
                dma(STG[sbi][:, k0:k1, c0:c1], src, [], [rSTG[sbi]], f"stg{sbi}")
            dstv = STGB[sbi].rearrange("p (k n) -> p k n", k=8)
            if scale is not None:
                eng = ["dve", "pool"][u % 2]
                tt(eng, dstv[:, 0:kcs, :], STG[sbi][:, 0:kcs, :], scale.unsqueeze(2).to_broadcast([128, kcs, 512]),
                   ALU.mult, [rSTG[sbi], rC], [rSTGB[sbi]])
            else:
                eng = cast_engs[u % 3]
                copy(eng, dstv[:, 0:kcs, :], STG[sbi][:, 0:kcs, :], [rSTG[sbi]], [rSTGB[sbi]])
            dma(wstream[u][:, 0:kcs * 512], STGB[sbi][:, 0:kcs * 512], [rSTGB[sbi]], [], f"stgb{sbi}")

        DGS = [SSDX[:, 0:2048], SSDX[:, 2048:4096]]
        rDGS = [res("DGS0"), res("DGS1")]
        for du in range(ND):
            b = du % 2
            dst = DGS[b].rearrange("p (m c) -> p m c", c=128)
            nmat = 16 if du < 6 else 12
            for mi in range(nmat):
                if du < 6:
                    jj, k = mi // 4, mi % 4
                    cc_ = convw_col(k, du * 4 + jj)
                else:
                    fu = du - 6
                    cc, k = mi // 3, mi % 3
                    ch = (2 * fu + cc) if cc < 2 else 22 + 2 * fu + (cc - 2)
                    cc_ = ffncw_col(k, ch)
                eng = ["dve", "pool"][mi % 2]
                tsc(eng, dst[:, mi, :], IDF[:], cc_, None, ALU.mult, None, [rC], [rDGS[b]])
            dma(dstream[du][:, 0:nmat * 128], DGS[b][:, 0:nmat * 128], [rDGS[b]], [], f"dgs{b}")

        T.barrier()
        for r in R.values():
            r.last_write = None
            r.readers = {}

        wpos = {"issued": 0, "dissued": 0}
        total_w = NT * NU
        total_d = NT * ND

        def w_ensure(upto):
            while wpos["issued"] < min(upto, total_w):
                q = wpos["issued"]
                u = q % NU
                s = q % NSLOT
                kind, arg = UNITS[u]
                kcs = 6 if (kind == "down" and arg[1] == 2) else 8
                dma(WSLOT[s][:].rearrange("p k n -> p (k n)")[:, 0:kcs * 512], wstream[u][:, 0:kcs * 512], [], [rWS[s]], f"ws{s}")
                wpos["issued"] += 1

        def d_ensure(upto):
            while wpos["dissued"] < min(upto, total_d):
                q = wpos["dissued"]
                du = q % ND
                s = q % NDSLOT
                nmat = 16 if du < 6 else 12
                dma(DSLOT[s][:].rearrange("p m c -> p (m c)")[:, 0:nmat * 128], dstream[du][:, 0:nmat * 128], [], [rDS[s]], f"ds{s}")
                wpos["dissued"] += 1

        wq = {"w": 0, "d": 0}

        def w_acquire(n=1):
            q = wq["w"]
            w_ensure(q + NSLOT)
            wq["w"] += n
            return [(WSLOT[(q + i) % NSLOT], rWS[(q + i) % NSLOT]) for i in range(n)]

        def d_acquire():
            q = wq["d"]
            d_ensure(q + NDSLOT)
            wq["d"] += 1
            return DSLOT[q % NDSLOT], rDS[q % NDSLOT]


        PB6 = [(PA[0], rPA[0]), (PA[1], rPA[1]), (SEG[0], rSEG[0]), (SEG[1], rSEG[1]), (PY[0], rPY[0]), (PY[1], rPY[1])]

        def pb6():
            return PB6[rot("pb6", 6)]

        def pb2():
            i = rot("pa", 2)
            return PA[i], rPA[i]

        def d_acquire_nopf():
            q = wq["d"]
            d_ensure(q + 1)
            wq["d"] += 1
            return DSLOT[q % NDSLOT], rDS[q % NDSLOT]

        def d_prefetch():
            d_ensure(wq["d"] + 1)

        def rms_chain(t4, c):
            jk = rot("junk", 2)
            act(JUNKS[jk][:], XH[:, t4, :], AF.Square, [rXH[t4]], [res(f"JUNK{jk}"), res(f"SSQ{c}")],
                accum_out=SSQ[:, c:c + 1])
            rsqrt_small(RSTD[:, c:c + 1], SSQ[:, c:c + 1], 1, 1.0 / 1024,
                        (res(f"SSQ{c}"), res(f"RSTD{c}"), res(f"MSQ{c}")), MSQ[:, c:c + 1])
            b = rot("xnb", 2)
            act(XNB[b][:], XH[:, t4, :], AF.Identity, [rXH[t4], res(f"RSTD{c}")], [rXNB[b]], scale=RSTD[:, c:c + 1])
            transposes([PT[:, j * 128:(j + 1) * 128] for j in range(8)],
                       [XNB[b][:, j * 128:(j + 1) * 128] for j in range(8)],
                       [rXNB[b]], [res("PT")], IDB[:])
            copy("dve", XNT[:, :, t4 * 128:(t4 + 1) * 128], PT[:].rearrange("p (j t) -> p j t", j=8),
                 [res("PT")], [rXNT[t4]])

        VN = XY[:].rearrange("p a b -> p (a b)")[:, 0:4096].rearrange("p (t c) -> p t c", t=4)
        MT = BC
        rSSDX = [res(f"YNB{g}") for g in range(4)] + [res("T1"), res("TS")]
        rSG = [res(f"SG{br}_{jj}") for br in range(2) for jj in range(4)]

        for m in range(NT):
            tok0 = m * 512
            if m == 0:
                for t4 in range(4):
                    dma(XH[:, t4, :], x[tok0 + t4 * 128:tok0 + (t4 + 1) * 128, :], [], [rXH[t4]], f"xh{t4}")
            T.handoff(rAT, ARENA_SSD)
            T.handoff(rMT, rBT + rCT)
            T.handoff(rSG, rSSDX)
            for t4 in range(4):
                rms_chain(t4, t4)

            T.handoff(rXY, rVN)
            for ub in range(2):
                (Wv, rWv), (Wu, rWu) = w_acquire(2)
                for i in range(4):
                    t4 = i
                    P, rP = pb6()
                    mm_group(P[:], [(XNT[:, kc, t4 * 128:(t4 + 1) * 128], Wv[:, kc, :]) for kc in range(8)],
                             [rXNT[t4], rWv], [rP])
                    vb = rot("vf", 4)
                    act(VF[vb][:], P[:], AF.Gelu, [rP], [rVF[vb]])
                    for g4 in range(4):
                        T.op("dve", lambda e, o=BST[vb % 2][:, g4, :], i_=VF[vb][:, g4 * 128:(g4 + 1) * 128]: e.bn_stats(out=o, in_=i_),
                             reads=[rVF[vb]], writes=[res(f"BST{vb % 2}_{g4}")])
                        T.op("dve", lambda e, o=MV[vb % 2][:, g4, :], i_=BST[vb % 2][:, g4, :]: e.bn_aggr(out=o, in_=i_),
                             reads=[res(f"BST{vb % 2}_{g4}")], writes=[res(f"MV{vb % 2}")])
                    tsc("dve", VE[vb % 2][:], MV[vb % 2][:, :, 1], EPS, None, ALU.add, None, [res(f"MV{vb % 2}")], [res(f"VE{vb % 2}")])
                    tt("pool", RSTDV[vb % 2][:], VE[vb % 2][:], NEGHALF[:, 0:4], ALU.pow, [res(f"VE{vb % 2}")], [res(f"RSTDV{vb % 2}")])
                    for g4 in range(4):
                        tsc("pool", VN[:, t4, ub * 512 + g4 * 128:ub * 512 + (g4 + 1) * 128], VF[vb][:, g4 * 128:(g4 + 1) * 128],
                            MV[vb % 2][:, g4, 0:1], RSTDV[vb % 2][:, g4:g4 + 1], ALU.subtract, ALU.mult,
                            [rVF[vb], res(f"MV{vb % 2}"), res(f"RSTDV{vb % 2}")], [rVN[t4]])
                    jj = i
                    j = ub * 4 + jj
                    P, rP = pb6()
                    mm_group(P[:], [(Wu[:, kc, jj * 128:(jj + 1) * 128], XNT[:, kc, :]) for kc in range(8)],
                             rXNT + [rWu], [rP])
                    act(UT[:, j, :], P[:], AF.Gelu, [rP], [rUT[j]])
            for g in range(8):
                P, rP = pb6()

                def fn(e, P=P, g=g):
                    ins = None
                    first = None
                    for t4 in range(4):
                        ins = e.matmul(P[:, t4 * 128:(t4 + 1) * 128], lhsT=VN[:, t4, g * 128:(g + 1) * 128],
                                       rhs=WST[:, g, :], start=True, stop=True)
                        if first is None:
                            first = ins
                    return first, ins
                T.op("pe", fn, reads=rVN, writes=[rP])
                tb = rot("tm", 2)
                stt(TM[tb][:].rearrange("p (t c) -> p t c", t=4), P[:].rearrange("p (t c) -> p t c", t=4),
                    lnw_col(g), QG[:, g, :].unsqueeze(1).to_broadcast([128, 4, 128]), ALU.mult, ALU.add,
                    [rP], [rTM[tb]])
                tt("pool", UT[:, g, :], TM[tb][:], UT[:, g, :], ALU.mult, [rTM[tb], rUT[g]], [rUT[g]])
            T.handoff(rVN, rXY)
            for ub in range(4):
                (W, rW), = w_acquire()
                for t4 in range(4):
                    P, rP = pb6()
                    mm_group(P[:], [(XNT[:, kc, t4 * 128:(t4 + 1) * 128], W[:, kc, :]) for kc in range(8)],
                             [rXNT[t4], rW], [rP])
                    act(ZS[:, t4, ub * 512:(ub + 1) * 512], P[:], AF.Silu, [rP], [rZS[t4]])
            def xconv_stage(j, jj, DG, rDG, pr):
                P2, rP2 = pb6()
                mm_group(P2[:], [(DG[:, jj * 4 + k, :], PRE[pr][:, 1 + k:1 + k + 512]) for k in range(4)],
                         [rDG, rPRE[pr]], [rP2])
                if j < 16:
                    dst, rd = XY[:, j, :], rXY
                elif j < 20:
                    dst, rd = BC[:, j - 16, :], rBT
                else:
                    dst, rd = BC[:, 4 + j - 20, :], rCT
                act(dst, P2[:], AF.Silu, [rP2], rd, bias=convb_col(j))

            pend = None
            for ub in range(6):
                (W, rW), = w_acquire()
                DG, rDG = d_acquire_nopf()
                for jj in range(4):
                    j = ub * 4 + jj
                    P, rP = pb6()
                    mm_group(P[:], [(W[:, kc, jj * 128:(jj + 1) * 128], XNT[:, kc, :]) for kc in range(8)],
                             rXNT + [rW], [rP])
                    pr = rot("pre", 3)
                    act(PRE[pr][:, 4:516], P[:], AF.Copy, [rP], [rPRE[pr]])
                    copy("pool", PRE[pr][:, 0:4], HISTX[:, j, :], [rHX[j]], [rPRE[pr]])
                    copy("pool", HISTX[:, j, :], PRE[pr][:, 512:516], [rPRE[pr]], [rHX[j]])
                    if pend is not None:
                        xconv_stage(*pend)
                    if jj == 0:
                        d_prefetch()
                    pend = (j, jj, DG, rDG, pr)
            xconv_stage(*pend)
            pend = None
            for t4 in range(4):
                mm_group(PM[:, 256:288], [(XNT[:, kc, t4 * 128:(t4 + 1) * 128], WDT[:, kc, :]) for kc in range(8)],
                         [rXNT[t4]], [res("PM")])
                tt("dve", DTR[:, t4, :], PM[:, 256:288], DTB_BC[:], ALU.add, [res("PM")], [res("DTR")])
            act(DTE[:], DTR[:], AF.Exp, [res("DTR")], [res("DTE")])
            act(DT[:], DTE[:], AF.Ln, [res("DTE")], [res("DT")], bias=1.0)
            tt("dve", DTA[:], DT[:], A_BC[:].unsqueeze(1).to_broadcast([128, 4, 32]), ALU.mult, [res("DT")], [res("DTA")])

            def S0(t4):
                c = {"xsb": rot("xstok", 2), "bb": rot("btok", 2), "eb": rot("eb", 2)}
                tsl = slice(t4 * 128, (t4 + 1) * 128)
                xsb, bb, eb = c["xsb"], c["bb"], c["eb"]
                for r2 in range(2):
                    transposes([PT[:, j * 128:(j + 1) * 128] for j in range(8)],
                               [XY[:, r2 * 8 + j, tsl] for j in range(8)], [rXY[t4]], [res("PT")], IDB[:])
                    copy("act" if r2 == 0 else "dve", XS_TOK[xsb][:, r2 * 1024:(r2 + 1) * 1024], PT[:], [res("PT")], [rXSTOK[xsb]])
                transposes([PT[:, g * 128:(g + 1) * 128] for g in range(4)],
                           [BC[:, g, tsl] for g in range(4)], [rBT[t4]], [res("PT")], IDB[:])
                copy("act", B_TOK[bb][:], PT[:, 0:512], [res("PT")], [rBTOK[bb]])

                def fn_scan(e, t4=t4):
                    first = e.matmul(PM[:, 256:288], lhsT=TRI[:], rhs=DTA[:, t4, :], start=True, stop=True)
                    e.matmul(PM[:, 288:320], lhsT=UPPER[:], rhs=DTA[:, t4, :], start=True, stop=True)
                    return first, e.matmul(PM[:, 320:352], lhsT=ONES[:], rhs=DTA[:, t4, :], start=True, stop=True)
                T.op("pe", fn_scan, reads=[res("DTA")], writes=[res("PM")])
                act(EXPS[eb][:], PM[:, 256:352], AF.Exp, [res("PM")], [res(f"EXPS{eb}")])
                tt("dve", DTD[eb][:], DT[:, t4, :], EXPS[eb][:, 32:64], ALU.mult, [res("DT"), res(f"EXPS{eb}")], [res(f"DTD{eb}")])
                for half in range(2):
                    def fn_cb(e, half=half, tsl=tsl):
                        ins = None
                        first = None
                        for gg in range(2):
                            g = half * 2 + gg
                            ins = e.matmul(PM[:, gg * 128:(gg + 1) * 128], lhsT=BC[:, g, tsl], rhs=BC[:, 4 + g, tsl],
                                           start=True, stop=True)
                            if first is None:
                                first = ins
                        return first, ins
                    T.op("pe", fn_cb, reads=[rBT[t4], rCT[t4]], writes=[res("PM")])
                    tt("dve", CBM[eb][:, half * 2:half * 2 + 2, :], PM[:, 0:256].rearrange("p (g l) -> p g l", g=2),
                       TRI[:].unsqueeze(1).to_broadcast([128, 2, 128]), ALU.mult, [res("PM")], [res(f"CBM{eb}")])
                return c

            def prep(t4, g, c):
                gsl = slice(g * 512, (g + 1) * 512)
                xsb, eb = c["xsb"], c["eb"]
                xg = rot("xg", 2)
                xsv = XS_TOK[xsb][:, gsl].rearrange("p (h c) -> p h c", h=8)
                tt("pool", XDT[xg][:].rearrange("p (h c) -> p h c", h=8), xsv,
                   DT[:, t4, g * 8:(g + 1) * 8].unsqueeze(2).to_broadcast([128, 8, 64]), ALU.mult,
                   [rXSTOK[xsb], res("DT")], [rXG[xg]])
                tt("pool", XDTD[xg][:].rearrange("p (h c) -> p h c", h=8), xsv,
                   DTD[eb][:, g * 8:(g + 1) * 8].unsqueeze(2).to_broadcast([128, 8, 64]), ALU.mult,
                   [rXSTOK[xsb], res(f"DTD{eb}")], [rXG[xg]])
                tt("pool", XSD[xg][:].rearrange("p (h c) -> p h c", h=8), xsv,
                   D_BC[:, g * 8:(g + 1) * 8].unsqueeze(2).to_broadcast([128, 8, 64]), ALU.mult,
                   [rXSTOK[xsb]], [rXG[xg]])
                xq = rot("xq", 2)
                tt("pool", XQ[xq][:].rearrange("p (h l) -> p h l", h=8),
                   DTA[:, t4, g * 8:(g + 1) * 8].unsqueeze(2).to_broadcast([128, 8, 128]),
                   TRI[:].unsqueeze(1).to_broadcast([128, 8, 128]), ALU.mult, [res("DTA")], [rXQ[xq]])
                return xg, xq

            def mainA(t4, g, c, xg, xq):
                tsl = slice(t4 * 128, (t4 + 1) * 128)
                gsl = slice(g * 512, (g + 1) * 512)
                bb, eb = c["bb"], c["eb"]
                yb = rot("py", 2)
                T.op("pe", lambda e, xg=xg, yb=yb: e.matmul(PY[yb][:], lhsT=IDB[:], rhs=XSD[xg][:], start=True, stop=False),
                     reads=[rXG[xg]], writes=[rPY[yb]])
                sgbs = []
                for hh in range(2):
                    sgb = rot("seg", 2)
                    sgbs.append(sgb)
                    T.op("pe", lambda e, sgb=sgb, xq=xq, hh=hh: e.matmul(SEG[sgb][:], lhsT=UPPER[:],
                                                                            rhs=XQ[xq][:, hh * 512:(hh + 1) * 512],
                                                                            start=True, stop=True),
                         reads=[rXQ[xq]], writes=[rSEG[sgb]])
                Pw, rPw = pb2()
                T.op("pe", lambda e, Pw=Pw, g=g, tsl=tsl, gsl=gsl: e.matmul(Pw[:], lhsT=BC[:, 4 + g, tsl], rhs=SBF[:, gsl],
                                                                               start=True, stop=True),
                     reads=[rCT[t4], rSBF[g]], writes=[rPw])
                Ps, rPs = pb2()
                T.op("pe", lambda e, Ps=Ps, bb=bb, g=g, xg=xg: e.matmul(Ps[:], lhsT=B_TOK[bb][:, g * 128:(g + 1) * 128],
                                                                           rhs=XDTD[xg][:], start=True, stop=True),
                     reads=[rBTOK[bb], rXG[xg]], writes=[rPs])
                for hh in range(2):
                    sgb = sgbs[hh]
                    db = rot("dec", 2)
                    act(DEC[db][:], SEG[sgb][:], AF.Exp, [rSEG[sgb]], [rDEC[db]])
                    gtb = rot("gt", 3)
                    tt("dve", GT[gtb][:], DEC[db][:].rearrange("p (h l) -> p h l", h=4),
                       CBM[eb][:, g, :].unsqueeze(1).to_broadcast([128, 4, 128]), ALU.mult,
                       [rDEC[db], res(f"CBM{eb}")], [rGT[gtb]])

                    def fn_y(e, gtb=gtb, hh=hh, xg=xg, yb=yb):
                        ins = None
                        first = None
                        for h4 in range(4):
                            h8 = hh * 4 + h4
                            ins = e.matmul(PY[yb][:, h8 * 64:(h8 + 1) * 64], lhsT=GT[gtb][:, h4, :],
                                           rhs=XDT[xg][:, h8 * 64:(h8 + 1) * 64], start=False, stop=(h8 == 7))
                            if first is None:
                                first = ins
                        return first, ins
                    T.op("pe", fn_y, reads=[rGT[gtb], rXG[xg]], writes=[rPY[yb]])
                return yb, Pw, rPw, Ps, rPs

            def mainB(t4, g, c, hA):
                yb, Pw, rPw, Ps, rPs = hA
                gsl = slice(g * 512, (g + 1) * 512)
                eb = c["eb"]
                tt("dve", T1[:].rearrange("p (h c) -> p h c", h=8), Pw[:].rearrange("p (h c) -> p h c", h=8),
                   EXPS[eb][:, g * 8:(g + 1) * 8].unsqueeze(2).to_broadcast([128, 8, 64]), ALU.mult,
                   [rPw, res(f"EXPS{eb}")], [res("T1")])
                yz = rot("yz", 2)
                tt("dve", YZ[yz][:], PY[yb][:], T1[:], ALU.add, [rPY[yb], res("T1")], [rYZ[yz]])
                tt("dve", YZ[yz][:], YZ[yz][:], ZS[:, t4, gsl], ALU.mult, [rYZ[yz], rZS[t4]], [rYZ[yz]])
                jk = rot("junk", 2)
                act(JUNKS[jk][:, 0:512], YZ[yz][:], AF.Square, [rYZ[yz]], [res(f"JUNK{jk}"), res(f"SSQG{g}")],
                    accum_out=SSQG[:, g:g + 1])
                rsqrt_small(RSTDG[:, g:g + 1], SSQG[:, g:g + 1], 1, 1.0 / 512,
                            (res(f"SSQG{g}"), res(f"RSTDG{g}"), res(f"MSG{g}")), MSG[:, g:g + 1])
                tsc("dve", YNB[:, gsl], YZ[yz][:], RSTDG[:, g:g + 1], None, ALU.mult, None,
                    [rYZ[yz], res(f"RSTDG{g}")], [res(f"YNB{g}")])
                tt("pool", TS[:].rearrange("p (h c) -> p h c", h=8), SST[:, gsl].rearrange("p (h c) -> p h c", h=8),
                   EXPS[eb][:, 64 + g * 8:64 + (g + 1) * 8].unsqueeze(2).to_broadcast([128, 8, 64]), ALU.mult,
                   [rSST[g], res(f"EXPS{eb}")], [res("TS")])
                tt("dve", SST[:, gsl], TS[:], Ps[:], ALU.add, [res("TS"), rPs], [rSST[g]])
                act(SBF[:, gsl], SST[:, gsl], AF.Copy, [rSST[g]], [rSBF[g]])

            def suffix(t4):
                tsl = slice(t4 * 128, (t4 + 1) * 128)
                for r2 in range(2):
                    transposes([PT[:, j * 128:(j + 1) * 128] for j in range(8)],
                               [YNB[:, (r2 * 8 + j) * 128:(r2 * 8 + j + 1) * 128] for j in range(8)],
                               [res(f"YNB{g}") for g in range(4)], [res("PT")], IDB[:])
                    copy("act" if r2 == 0 else "dve", XY[:, r2 * 8:(r2 + 1) * 8, tsl], PT[:].rearrange("p (j t) -> p j t", j=8),
                         [res("PT")], [rXY[t4]])

            order = [(a, b) for a in range(4) for b in range(4)]
            ctxs = {0: S0(0)}
            pp = prep(0, 0, ctxs[0])
            pending_suffix = None
            for idx, (t4, g) in enumerate(order):
                hA = mainA(t4, g, ctxs[t4], *pp)
                if g == 0 and pending_suffix is not None:
                    suffix(pending_suffix)
                    pending_suffix = None
                if g == 1 and t4 + 1 < 4:
                    ctxs[t4 + 1] = S0(t4 + 1)
                if idx + 1 < 16:
                    nt_, ng_ = order[idx + 1]
                    pp = prep(nt_, ng_, ctxs[nt_])
                mainB(t4, g, ctxs[t4], hA)
                if g == 3:
                    pending_suffix = t4
            suffix(3)

            T.handoff(rBT + rCT, rMT)
            T.handoff(rSSDX, rSG)
            for jb in range(2):
                for br, SG in ((0, SGA), (1, SGB)):
                    (W, rW), = w_acquire()
                    for jj in range(4):
                        P, rP = pb6()
                        mm_group(P[:], [(W[:, kc, jj * 128:(jj + 1) * 128], XNT[:, kc, :]) for kc in range(8)],
                                 rXNT + [rW], [rP])
                        act(SG[:, jj, :], P[:], AF.Sigmoid, [rP], [res(f"SG{br}_{jj}")], bias=gb_col(br, jb * 4 + jj))
                (Wa, rWa), (Wb0, rWb0), (Wb1, rWb1) = w_acquire(3)
                for jj in range(4):
                    j = jb * 4 + jj
                    P, rP = pb6()
                    mm_group(P[:], [(Wa[:, kc, jj * 128:(jj + 1) * 128], UT[:, kc, :]) for kc in range(8)],
                             rUT + [rWa], [rP])
                    tb = rot("tm", 2)
                    tt("dve", TM[tb][:], P[:], SGA[:, jj, :], ALU.mult, [rP, res(f"SG0_{jj}")], [rTM[tb]])
                    P2, rP2 = pb6()
                    mm_group(P2[:], [((Wb0 if kc < 8 else Wb1)[:, kc % 8, jj * 128:(jj + 1) * 128], XY[:, kc, :])
                                     for kc in range(16)], rXY + [rWb0, rWb1], [rP2])
                    tb2 = rot("tm", 2)
                    tt("dve", TM[tb2][:], P2[:], SGB[:, jj, :], ALU.mult, [rP2, res(f"SG1_{jj}")], [rTM[tb2]])
                    tt("pool", MT[:, j, :], TM[tb][:], TM[tb2][:], ALU.add, [rTM[tb], rTM[tb2]], [rMT[j]])
            for nb in range(2):
                (W, rW), = w_acquire()
                for t4 in range(4):
                    P, rP = pb6()
                    mm_group(P[:], [(MT[:, kc, t4 * 128:(t4 + 1) * 128], W[:, kc, :]) for kc in range(8)],
                             rMT + [rW], [rP])
                    tt("dve", XH[:, t4, nb * 512:(nb + 1) * 512], P[:], XH[:, t4, nb * 512:(nb + 1) * 512], ALU.add,
                       [rP, rXH[t4]], [rXH[t4]])
                    if nb == 1:
                        if dbg:
                            dma(dbg_out[tok0 + t4 * 128:tok0 + (t4 + 1) * 128, :], XH[:, t4, :], [rXH[t4]], [], f"dbg{t4}")
                        rms_chain(t4, 4 + t4)
            T.handoff(ARENA_SSD, rAT)

            def fconv_stage(fu, cc, ch, DG, rDG, pr):
                P2, rP2 = pb6()
                mm_group(P2[:], [(DG[:, cc * 3 + k, :], PRE[pr][:, k:k + 512]) for k in range(3)],
                         [rDG, rPRE[pr]], [rP2])
                if cc < 2:
                    act(GS[cc][:], P2[:], AF.Silu, [rP2], [rGS[cc]], bias=ffnb_col(ch))
                else:
                    jat = 2 * fu + (cc - 2)
                    stt(AT[:, jat, :], P2[:], ffnb_col(ch), GS[cc - 2][:], ALU.add, ALU.mult,
                        [rP2, rGS[cc - 2]], [rAT[jat]])

            pend = None
            for fu in range(11):
                (W, rW), = w_acquire()
                DG, rDG = d_acquire_nopf()
                for cc in range(4):
                    ch = (2 * fu + cc) if cc < 2 else 22 + 2 * fu + (cc - 2)
                    P, rP = pb6()
                    mm_group(P[:], [(W[:, kc, cc * 128:(cc + 1) * 128], XNT[:, kc, :]) for kc in range(8)],
                             rXNT + [rW], [rP])
                    pr = rot("pre", 3)
                    act(PRE[pr][:, 2:514], P[:], AF.Copy, [rP], [rPRE[pr]])
                    copy("pool", PRE[pr][:, 0:2], HISTF[:, ch, :], [rHF[ch]], [rPRE[pr]])
                    copy("pool", HISTF[:, ch, :], PRE[pr][:, 512:514], [rPRE[pr]], [rHF[ch]])
                    if pend is not None:
                        fconv_stage(*pend)
                    if cc == 0:
                        d_prefetch()
                    pend = (fu, cc, ch, DG, rDG, pr)
            fconv_stage(*pend)
            pend = None
            for nb in range(2):
                slots = w_acquire(3)
                for t4 in range(4):
                    P, rP = pb6()
                    pairs = []
                    for kc in range(22):
                        Wk, _ = slots[kc // 8]
                        pairs.append((AT[:, kc, t4 * 128:(t4 + 1) * 128], Wk[:, kc % 8, :]))
                    mm_group(P[:], pairs, rAT + [s[1] for s in slots], [rP])
                    tt("dve", XH[:, t4, nb * 512:(nb + 1) * 512], P[:], XH[:, t4, nb * 512:(nb + 1) * 512], ALU.add,
                       [rP, rXH[t4]], [rXH[t4]])
                    if nb == 1:
                        c = 8 + t4
                        jk = rot("junk", 2)
                        act(JUNKS[jk][:], XH[:, t4, :], AF.Square, [rXH[t4]], [res(f"JUNK{jk}"), res(f"SSQ{c}")],
                            accum_out=SSQ[:, c:c + 1])
                        rsqrt_small(RSTD[:, c:c + 1], SSQ[:, c:c + 1], 1, 1.0 / 1024,
                                    (res(f"SSQ{c}"), res(f"RSTD{c}"), res(f"MSQ{c}")), MSQ[:, c:c + 1])
                        stt(XH[:, t4, :], XH[:, t4, :], RSTD[:, c:c + 1], FNW_BC[:], ALU.mult, ALU.mult,
                            [rXH[t4], res(f"RSTD{c}")], [rXH[t4]])
                        dma(out[tok0 + t4 * 128:tok0 + (t4 + 1) * 128, :], XH[:, t4, :], [rXH[t4]], [], f"xo{t4}")
                        if m + 1 < NT:
                            nt0 = tok0 + 512
                            dma(XH[:, t4, :], x[nt0 + t4 * 128:nt0 + (t4 + 1) * 128, :], [], [rXH[t4]], f"xh{t4}")


        T.final_wait_all_dma("sp")

        sem_es = ExitStack()
        with sem_es:
            engsems = {e: sem_es.enter_context(nc.semaphore(f"s_{e}")) for e in ENGS if e != "sp"}
            for k, d in T.dmasems.items():
                d.handle = sem_es.enter_context(nc.semaphore(f"d_{k}"))
            with nc.Block() as block:
                T.emit(nc, block, engsems)
    return nc


def _consts():
    i = np.arange(128)
    ident = np.eye(128, dtype=np.float32)
    tri = (i[:, None] <= i[None, :]).astype(np.float32)
    upper = (i[:, None] > i[None, :]).astype(np.float32)
    ones = np.ones((128, 128), np.float32)
    cid = i // 64
    gmask = (cid[None, :] <= cid[:, None]).astype(np.float32)
    return dict(c_ident=ident, c_tri=tri, c_upper=upper, c_ones=ones, c_gmask=gmask)


def _core_inputs(inp, b, S):
    f = lambda a: np.ascontiguousarray(np.asarray(a, dtype=np.float32))
    d = dict(
        x=f(inp["x"][b, :S]),
        w_in=f(inp["w_in"][0]),
        mix_norm_w=f(inp["mix_norm_w"][0]).reshape(8, 128),
        gate_bias=f(inp["gate_bias"][0]).reshape(16, 128),
        gmlp_ln_w=f(inp["gmlp_ln_w"][0]).reshape(8, 128),
        gmlp_ln_b=f(inp["gmlp_ln_b"][0]).reshape(1, 1024),
        gmlp_ws=f(inp["gmlp_ws"][0]),
        gmlp_bs=f(inp["gmlp_bs"][0]).reshape(1, 1024),
        ssm_conv_w=f(inp["ssm_conv_w"][0]).reshape(96, 128),
        ssm_conv_b=f(inp["ssm_conv_b"][0]).reshape(24, 128),
        ssm_dt_bias=f(inp["ssm_dt_bias"][0]).reshape(1, 32),
        ssm_a_log=f(inp["ssm_a_log"][0]).reshape(1, 32),
        ssm_d=f(inp["ssm_d"][0]).reshape(1, 32),
        ssm_norm_w=f(inp["ssm_norm_w"][0]).reshape(16, 128),
        w_proj_a=f(inp["w_proj_a"][0]),
        w_proj_b=f(inp["w_proj_b"][0]),
        w_out=f(inp["w_out"][0]),
        ffn_norm_w=f(inp["ffn_norm_w"][0]).reshape(8, 128),
        ffn_w_up=f(inp["ffn_w_up"][0]),
        ffn_conv_w=f(inp["ffn_conv_w"][0]).reshape(132, 128),
        ffn_conv_b=f(inp["ffn_conv_b"][0]).reshape(44, 128),
        ffn_w_down=f(inp["ffn_w_down"][0]),
        final_norm_w=f(inp["final_norm_w"]).reshape(1, 1024),
    )
    d.update(_consts())
    return d


def run(inputs, S=None, n_cores=8, dbg=False, trace=False):
    B = inputs["x"].shape[0]
    if S is None:
        S = inputs["x"].shape[1]
    nc = build(S, dbg=dbg)
    in_maps = [_core_inputs(inputs, b, S) for b in range(n_cores)]
    res = run_bass_kernel_spmd(nc, in_maps, core_ids=list(range(n_cores)), trace=trace)
    outs = np.stack([np.asarray(r["out"], dtype=np.float32) for r in res.results], axis=0)
    if dbg:
        return outs, np.stack([np.asarray(r["dbg"], dtype=np.float32) for r in res.results], axis=0), res
    return outs


def kernel(**inputs):
    return run(inputs, n_cores=inputs["x"].shape[0])
```
